# Optimizing a Trainium2 kernel written in Bass

```python
import jax, jax.numpy as jnp
from jax import lax
import numpy as np

D_MODEL = 1024
BATCH = 4
SEQ = 8192
DEPTH = 4

N_MIXERS = 4
N_META = 16
Q_BLOCK = 128
EPS = 1e-6
POOL_WINDOWS = (2, 4, 8, 16)
N_POOL_GROUPS = len(POOL_WINDOWS)
POOL_GROUP = D_MODEL // N_POOL_GROUPS
N_HEADS = 16
HEAD_DIM = D_MODEL // N_HEADS
MLA_HEADS = 16
MLA_Q_RANK = 384
MLA_KV_RANK = 256
MLA_NOPE = 64
MLA_ROPE = 32
MLA_V = 64
ROPE_THETA = 10000.0
D_FF = ((-(-8 * D_MODEL // 3) + 255) // 256) * 256

kernel_name = "hybrid_pool_sb_mla_fox_trunk"


def _n_layers_of(m):
    return len(range(m, DEPTH, N_MIXERS))


def rmsnorm(x, g):
    xf = x.astype(jnp.float32)
    y = xf * lax.rsqrt(jnp.mean(xf * xf, axis=-1, keepdims=True) + EPS)
    return (y * g.astype(jnp.float32)).astype(x.dtype)


def swiglu(h, w_gate, w_up, w_down):
    return (jax.nn.silu(h @ w_gate) * (h @ w_up)) @ w_down


def sweep_queries(attend, q_parts, kv_parts):
    L = q_parts[0].shape[1]
    pos = jnp.arange(L)
    meta_out = attend(tuple(a[:, :N_META] for a in q_parts), pos[:N_META],
                      tuple(a[:, :N_META] for a in kv_parts), pos[:N_META])
    n_blk = (L - N_META) // Q_BLOCK

    def body(i):
        start = N_META + i * Q_BLOCK
        qs = tuple(lax.dynamic_slice_in_dim(a, start, Q_BLOCK, axis=1) for a in q_parts)
        return attend(qs, start + jnp.arange(Q_BLOCK), kv_parts, pos)

    out = lax.map(body, jnp.arange(n_blk))
    B = out.shape[1]
    out = jnp.moveaxis(out, 0, 1).reshape((B, n_blk * Q_BLOCK) + out.shape[3:])
    return jnp.concatenate([meta_out, out], axis=1)


def softmax_block(q, k, v, qpos, kpos, scale, q_decay=None, k_decay=None):
    s = jnp.einsum('bqhd,bkhd->bhqk', q, k).astype(jnp.float32) * scale
    if q_decay is not None:
        s = s + (jnp.transpose(q_decay, (0, 2, 1))[:, :, :, None]
                 - jnp.transpose(k_decay, (0, 2, 1))[:, :, None, :]).astype(jnp.float32)
    mask = kpos[None, :] <= qpos[:, None]
    s = jnp.where(mask, s, jnp.finfo(jnp.float32).min)
    p = jax.nn.softmax(s, axis=-1)
    return jnp.einsum('bhqk,bkhd->bqhd', p.astype(v.dtype), v)


def pool_mixer(h, w, scale):
    B, L, _ = h.shape
    hf = h.astype(jnp.float32)
    pos = jnp.arange(L)
    outs = []
    for g, win in enumerate(POOL_WINDOWS):
        xg = hf[..., g * POOL_GROUP:(g + 1) * POOL_GROUP]
        cs = jnp.cumsum(xg, axis=1)
        lag = jnp.pad(cs[:, :-win], ((0, 0), (win, 0), (0, 0)))
        cnt = jnp.minimum(pos + 1, win).astype(jnp.float32)[None, :, None]
        outs.append((cs - lag) / cnt - xg)
    pooled = jnp.stack(outs, axis=2).astype(h.dtype)
    mixed = jnp.einsum('blgc,gcd->blgd', pooled, w).reshape(B, L, D_MODEL)
    return mixed * scale


def _sb_attend(qs, qpos, kvs, kpos):
    (q,) = qs
    k, v = kvs
    z = jnp.einsum('bqhd,bkhd->bhqk', q, k).astype(jnp.float32) * (HEAD_DIM ** -0.5)
    mask = kpos[None, :] < qpos[:, None]
    log_keep = jnp.where(mask, jax.nn.log_sigmoid(-z), 0.0)
    later = lax.cumsum(log_keep, axis=3, reverse=True) - log_keep
    a = jnp.where(mask, jnp.exp(jax.nn.log_sigmoid(z) + later), 0.0)
    return jnp.einsum('bhqk,bkhd->bqhd', a.astype(v.dtype), v)


def sb_mixer(h, w_qkv, w_o):
    B, L, _ = h.shape
    qkv = (h @ w_qkv).reshape(B, L, 3, N_HEADS, HEAD_DIM)
    q, k, v = qkv[:, :, 0], qkv[:, :, 1], qkv[:, :, 2]
    o = sweep_queries(_sb_attend, (q,), (k, v))
    return o.reshape(B, L, N_HEADS * HEAD_DIM) @ w_o


def _rope(x, cos, sin):
    xf = x.astype(jnp.float32)
    half = xf.shape[-1] // 2
    x1, x2 = xf[..., :half], xf[..., half:]
    return jnp.concatenate([x1 * cos - x2 * sin, x2 * cos + x1 * sin], axis=-1).astype(x.dtype)


def _mla_attend(qs, qpos, kvs, kpos):
    (q,) = qs
    k, v = kvs
    return softmax_block(q, k, v, qpos, kpos, (MLA_NOPE + MLA_ROPE) ** -0.5)


def mla_mixer(h, w_down, q_norm, kv_norm, w_uq, w_ukv, w_o):
    B, L, _ = h.shape
    down = h @ w_down
    c_q = rmsnorm(down[..., :MLA_Q_RANK], q_norm)
    c_kv = rmsnorm(down[..., MLA_Q_RANK:MLA_Q_RANK + MLA_KV_RANK], kv_norm)
    k_rope = down[..., MLA_Q_RANK + MLA_KV_RANK:]
    q = (c_q @ w_uq).reshape(B, L, MLA_HEADS, MLA_NOPE + MLA_ROPE)
    kv = (c_kv @ w_ukv).reshape(B, L, MLA_HEADS, MLA_NOPE + MLA_V)
    q_nope, q_rope = q[..., :MLA_NOPE], q[..., MLA_NOPE:]
    k_nope, v = kv[..., :MLA_NOPE], kv[..., MLA_NOPE:]
    inv = ROPE_THETA ** (-jnp.arange(0, MLA_ROPE, 2, dtype=jnp.float32) / MLA_ROPE)
    ang = jnp.arange(L, dtype=jnp.float32)[:, None] * inv[None, :]
    cos, sin = jnp.cos(ang), jnp.sin(ang)
    q_rope = _rope(q_rope, cos[:, None, :], sin[:, None, :])
    k_rope = _rope(k_rope, cos, sin)
    q = jnp.concatenate([q_nope, q_rope], axis=-1)
    k = jnp.concatenate([k_nope, jnp.broadcast_to(k_rope[:, :, None, :], (B, L, MLA_HEADS, MLA_ROPE))], axis=-1)
    o = sweep_queries(_mla_attend, (q,), (k, v))
    return o.reshape(B, L, MLA_HEADS * MLA_V) @ w_o


def _fox_attend(qs, qpos, kvs, kpos):
    q, fq = qs
    k, v, fk = kvs
    return softmax_block(q, k, v, qpos, kpos, HEAD_DIM ** -0.5, fq, fk)


def fox_mixer(h, w_qkvf, b_f, w_o):
    B, L, _ = h.shape
    proj = h @ w_qkvf
    qkv = proj[..., :3 * N_HEADS * HEAD_DIM].reshape(B, L, 3, N_HEADS, HEAD_DIM)
    q, k, v = qkv[:, :, 0], qkv[:, :, 1], qkv[:, :, 2]
    f_logit = proj[..., 3 * N_HEADS * HEAD_DIM:].astype(jnp.float32) + b_f.astype(jnp.float32)
    F = jnp.cumsum(jax.nn.log_sigmoid(f_logit), axis=1)
    o = sweep_queries(_fox_attend, (q, F), (k, v, F))
    return o.reshape(B, L, N_HEADS * HEAD_DIM) @ w_o


def setup_inputs(seed: int = 0) -> dict:
    key = jax.random.key(seed)
    ks = jax.random.split(key, 24)
    f32 = jnp.float32

    def w(k, shape, fan_in):
        return jax.random.normal(k, shape, f32) * (fan_in ** -0.5)

    def gain(k, shape):
        return 1.0 + 0.02 * jax.random.normal(k, shape, f32)

    nA, nB, nC, nD = (_n_layers_of(m) for m in range(N_MIXERS))
    D = D_MODEL
    return {
        "x": jax.random.normal(ks[0], (BATCH, SEQ, D), f32),
        "meta": jax.random.normal(ks[1], (N_META, D), f32),
        "norm_mix": gain(ks[2], (DEPTH, D)),
        "norm_ffn": gain(ks[3], (DEPTH, D)),
        "pool_w": w(ks[4], (nA, N_POOL_GROUPS, POOL_GROUP, POOL_GROUP), POOL_GROUP),
        "pool_scale": gain(ks[5], (nA, D)),
        "sb_w_qkv": w(ks[6], (nB, D, 3 * N_HEADS * HEAD_DIM), D),
        "sb_w_o": w(ks[7], (nB, N_HEADS * HEAD_DIM, D), N_HEADS * HEAD_DIM),
        "mla_w_down": w(ks[8], (nC, D, MLA_Q_RANK + MLA_KV_RANK + MLA_ROPE), D),
        "mla_q_norm": gain(ks[9], (nC, MLA_Q_RANK)),
        "mla_kv_norm": gain(ks[10], (nC, MLA_KV_RANK)),
        "mla_w_uq": w(ks[11], (nC, MLA_Q_RANK, MLA_HEADS * (MLA_NOPE + MLA_ROPE)), MLA_Q_RANK),
        "mla_w_ukv": w(ks[12], (nC, MLA_KV_RANK, MLA_HEADS * (MLA_NOPE + MLA_V)), MLA_KV_RANK),
        "mla_w_o": w(ks[13], (nC, MLA_HEADS * MLA_V, D), MLA_HEADS * MLA_V),
        "fox_w_qkvf": w(ks[14], (nD, D, 3 * N_HEADS * HEAD_DIM + N_HEADS), D),
        "fox_b_f": 2.0 + 0.5 * jax.random.normal(ks[15], (nD, N_HEADS), f32),
        "fox_w_o": w(ks[16], (nD, N_HEADS * HEAD_DIM, D), N_HEADS * HEAD_DIM),
        "ffn_w_gate": w(ks[17], (DEPTH, D, D_FF), D),
        "ffn_w_up": w(ks[18], (DEPTH, D, D_FF), D),
        "ffn_w_down": w(ks[19], (DEPTH, D_FF, D), D_FF),
        "final_norm": gain(ks[20], (D,)),
    }


def reference(x, meta, norm_mix, norm_ffn, pool_w, pool_scale, sb_w_qkv, sb_w_o,
              mla_w_down, mla_q_norm, mla_kv_norm, mla_w_uq, mla_w_ukv, mla_w_o,
              fox_w_qkvf, fox_b_f, fox_w_o, ffn_w_gate, ffn_w_up, ffn_w_down, final_norm):
    B = x.shape[0]
    meta_b = jnp.broadcast_to(meta[None].astype(x.dtype), (B, N_META, D_MODEL))
    h = jnp.concatenate([meta_b, x], axis=1)
    for i in range(DEPTH):
        m, j = i % N_MIXERS, i // N_MIXERS
        a = rmsnorm(h, norm_mix[i])
        if m == 0:
            mix = pool_mixer(a, pool_w[j], pool_scale[j])
        elif m == 1:
            mix = sb_mixer(a, sb_w_qkv[j], sb_w_o[j])
        elif m == 2:
            mix = mla_mixer(a, mla_w_down[j], mla_q_norm[j], mla_kv_norm[j],
                            mla_w_uq[j], mla_w_ukv[j], mla_w_o[j])
        else:
            mix = fox_mixer(a, fox_w_qkvf[j], fox_b_f[j], fox_w_o[j])
        h = h + mix
        h = h + swiglu(rmsnorm(h, norm_ffn[i]), ffn_w_gate[i], ffn_w_up[i], ffn_w_down[i])
    h = rmsnorm(h, final_norm)
    return h[:, N_META:]
```

```python
import os
import numpy as np
from contextlib import ExitStack
import ml_dtypes
import concourse.bass as bass
import concourse.mybir as mybir
from concourse.bass_utils import run_bass_kernel_spmd

F32 = mybir.dt.float32
BF16 = mybir.dt.bfloat16
AF = mybir.ActivationFunctionType
ALU = mybir.AluOpType

D = 1024
DFF = 2816
NM = DFF // 128
SEQ = 8192
NMETA = 16
L = SEQ + NMETA
NT = L // 2
NCH = 8
TAIL = 8
EPS = 1e-6
NEG = -30000.0


class Buf:
    __slots__ = ("name", "w", "r")

    def __init__(self, name=""):
        self.name = name
        self.w = None
        self.r = []


class Op:
    __slots__ = ("eng", "fn", "deps", "need_inc", "val", "sem", "is_dma", "idx", "epoch")


COMPUTE = ("pe", "act", "dve", "pool")
QUEUES = ("sp", "pq", "cc")
QINC = {"sp": 16, "pq": 16, "cc": 1}


def _stream(eng):
    return "pool" if eng in ("pq", "cc") else eng


class Prog:
    def __init__(self, nc, n_dma_sems=8):
        self.nc = nc
        self.ops = []
        self.n_dma_sems = n_dma_sems
        self.last = {}
        self.dma_since = []
        self.pending = {}
        self.epoch = 0
        self.ep_cnt = {}

    def op(self, eng, fn, reads=(), writes=()):
        o = Op()
        o.eng = eng
        o.fn = fn
        o.is_dma = eng in QUEUES
        o.need_inc = o.is_dma
        o.val = None
        o.sem = None
        o.idx = len(self.ops)
        o.epoch = self.epoch
        self.ep_cnt[eng] = self.ep_cnt.get(eng, 0) + 1
        deps = []
        for b in reads:
            if b.w is not None:
                deps.append(b.w)
        for b in writes:
            if b.w is not None:
                deps.append(b.w)
            deps.extend(b.r)
        st = _stream(eng)
        if st in self.pending:
            deps.extend(self.pending.pop(st))
        for b in reads:
            b.r = [x for x in b.r if x.is_dma or _stream(x.eng) != st] + [o]
        for b in writes:
            b.w = o
            b.r = []
        seen = set()
        dd = []
        for d in deps:
            if d.idx not in seen:
                seen.add(d.idx)
                dd.append(d)
        o.deps = dd
        self.ops.append(o)
        self.last[st] = o
        if o.is_dma:
            self.dma_since.append(o)
        return o

    def barrier(self):
        if self.ep_cnt and max(self.ep_cnt.values()) > 20000:
            self.epoch += 1
            self.ep_cnt = {}
        deps = list(self.last.values()) + list(self.dma_since)
        self.dma_since = []
        for st in ("pe", "act", "dve", "pool", "sp"):
            self.pending[st] = list(self.pending.get(st, [])) + deps

    def emit(self, final_wait_ops=()):
        nc = self.nc
        ops = self.ops
        for o in ops:
            so = _stream(o.eng)
            for d in o.deps:
                if d.is_dma:
                    continue
                if _stream(d.eng) == so and so == "pe":
                    continue
                d.need_inc = True
        with ExitStack() as stack:
            sems = {(e, ep): stack.enter_context(nc.semaphore("s_%s%d" % (e, ep))) for e in COMPUTE for ep in range(self.epoch + 1)}
            nds = {q: (1 if q == "cc" else self.n_dma_sems) for q in QUEUES}
            dsems = {q: [stack.enter_context(nc.semaphore("d_%s%d" % (q, i))) for i in range(nds[q])]
                     for q in QUEUES}
            cnt = {k: 0 for k in sems}
            dcnt = {q: [0] * nds[q] for q in QUEUES}
            drr = {q: 0 for q in QUEUES}
            per = {s: [] for s in ("pe", "act", "dve", "pool", "sp")}
            for o in ops:
                if o.is_dma:
                    i = drr[o.eng]
                    drr[o.eng] = (i + 1) % nds[o.eng]
                    dcnt[o.eng][i] += QINC[o.eng]
                    o.sem = dsems[o.eng][i]
                    o.val = dcnt[o.eng][i]
                elif o.need_inc:
                    cnt[(o.eng, o.epoch)] += 1
                    o.sem = sems[(o.eng, o.epoch)]
                    o.val = cnt[(o.eng, o.epoch)]
                per[_stream(o.eng)].append(o)
            block = stack.enter_context(nc.Block())

            def run(sname, eng_obj):
                waited = {}
                for o in per[sname]:
                    ws = []
                    for d in o.deps:
                        if (not d.is_dma) and _stream(d.eng) == sname and sname == "pe":
                            continue
                        ws.append((d.sem, d.val))
                    if o.is_dma and o.val > QINC[o.eng]:
                        ws.append((o.sem, o.val - QINC[o.eng]))
                    for (s, v) in ws:
                        k = id(s)
                        if waited.get(k, 0) >= v:
                            continue
                        waited[k] = v
                        eng_obj.wait_ge(s, v)
                    ins = o.fn(eng_obj)
                    if o.need_inc:
                        ins.then_inc(o.sem, QINC[o.eng] if o.is_dma else 1)
                if sname == "sp":
                    fin = {}
                    for o in final_wait_ops:
                        if o.val > fin.get(id(o.sem), (None, 0))[1]:
                            fin[id(o.sem)] = (o.sem, o.val)
                    for (s, v) in fin.values():
                        eng_obj.wait_ge(s, v)

            @block.tensor
            def _(e):
                run("pe", e)

            @block.scalar
            def _(e):
                run("act", e)

            @block.vector
            def _(e):
                run("dve", e)

            @block.gpsimd
            def _(e):
                run("pool", e)

            @block.sync
            def _(e):
                run("sp", e)
        return nc


class Tn:
    __slots__ = ("t", "b")

    def __init__(self, t, name=""):
        self.t = t
        self.b = Buf(name)

    def __getitem__(self, k):
        return self.t[k]


def _bufs(lst):
    return [x.b if isinstance(x, Tn) else x for x in lst]


class Ctx:
    def __init__(self):
        self.nc = bass.Bass("TRN2", target_bir_lowering=False)
        self.P = Prog(self.nc)
        self.outs = []
        self.uid = 0
        self.ext_in = []
        self.ext_out = []

    def name(self, base):
        self.uid += 1
        return "%s_%d" % (base, self.uid)

    def din(self, name, shape, dt=F32):
        self.ext_in.append(name)
        return self.nc.dram_tensor(name, list(shape), dt, kind="ExternalInput").ap()

    def dout(self, name, shape, dt=F32):
        self.ext_out.append(name)
        return self.nc.dram_tensor(name, list(shape), dt, kind="ExternalOutput").ap()

    def dint(self, name, shape, dt=F32):
        return self.nc.dram_tensor(name, list(shape), dt, kind="Internal").ap()

    def sb(self, st, base, shape, dt):
        n = self.name(base)
        return Tn(st.enter_context(self.nc.sbuf_tensor(n, list(shape), dt)), n)

    def ps(self, st, base, shape, dt):
        n = self.name(base)
        return Tn(st.enter_context(self.nc.psum_tensor(n, list(shape), dt)), n)

    def mm(self, out, lhsT, rhs, start, stop, r, w):
        return self.P.op("pe", lambda e: e.matmul(out, lhsT=lhsT, rhs=rhs, start=start, stop=stop), _bufs(r), _bufs(w))

    def tr(self, out, in_, ident, r, w):
        return self.P.op("pe", lambda e: e.transpose(out, in_, ident), _bufs(r), _bufs(w))

    def act(self, out, in_, func, r, w, **kw):
        return self.P.op("act", lambda e: e.activation(out=out, in_=in_, func=func, **kw), _bufs(r), _bufs(w))

    def tt(self, eng, out, in0, in1, op, r, w):
        return self.P.op(eng, lambda e: e.tensor_tensor(out=out, in0=in0, in1=in1, op=op), _bufs(r), _bufs(w))

    def ts(self, eng, out, in0, s1, s2, op0, op1, r, w):
        if op1 is None:
            return self.P.op(eng, lambda e: e.tensor_scalar(out=out, in0=in0, scalar1=s1, scalar2=None, op0=op0), _bufs(r), _bufs(w))
        return self.P.op(eng, lambda e: e.tensor_scalar(out=out, in0=in0, scalar1=s1, scalar2=s2, op0=op0, op1=op1), _bufs(r), _bufs(w))

    def stt(self, eng, out, in0, scalar, in1, op0, op1, r, w):
        return self.P.op(eng, lambda e: e.scalar_tensor_tensor(out=out, in0=in0, scalar=scalar, in1=in1, op0=op0, op1=op1), _bufs(r), _bufs(w))

    def cp(self, eng, out, in_, r, w):
        if eng == "act":
            return self.act(out, in_, AF.Copy, r, w)
        return self.P.op(eng, lambda e: e.tensor_copy(out=out, in_=in_), _bufs(r), _bufs(w))

    def memset(self, eng, ap, val, w):
        return self.P.op(eng, lambda e: e.memset(ap, val), [], _bufs(w))

    def recip(self, out, in_, r, w):
        return self.P.op("dve", lambda e: e.reciprocal(out=out, in_=in_), _bufs(r), _bufs(w))

    def dma(self, q, out, in_, r=(), w=(), final=False):
        o = self.P.op(q, lambda e: e.dma_start(out=out, in_=in_), _bufs(r), _bufs(w))
        if final:
            self.outs.append(o)
        return o

    def finish(self):
        self.P.barrier()
        self.P.emit(final_wait_ops=self.outs)
        return self.nc


def chunk_list():
    return [(j * 512, 512) for j in range(NCH)] + [(NCH * 512, TAIL)]


def make_consts(cx, st):
    c = {}
    ident = cx.sb(st, "ident", [128, 128], BF16)
    cx.memset("pool", ident[:], 1.0, [ident])
    cx.P.op("pool", lambda e: e.affine_select(out=ident[:], in_=ident[:], pattern=[[-1, 128]],
                                              compare_op=ALU.is_equal, fill=0.0, base=0, channel_multiplier=1),
            [ident.b], [ident.b])
    c["ident"] = ident
    return c


def load_bcast(cx, st, name, vec_ap, n):
    t = cx.sb(st, name, [128, n], F32)
    cx.dma("sp", t[:], vec_ap.rearrange("(o n) -> o n", o=1).to_broadcast([128, n]), w=[t])
    return t


def rms_rows(cx, sc, src_ap, src_bufs, nrows, ncols, gb, out_ap, out_bufs, eng_sq="act"):
    junk, ss, rstd = sc["junk"], sc["ss"], sc["rstd"]
    cx.memset("dve", ss[0:nrows, :], 0.0, [ss])
    cx.act(junk[0:nrows, 0:ncols], src_ap, AF.Square, list(src_bufs) + [ss], [junk, ss], accum_out=ss[0:nrows, :])
    cx.ts("dve", rstd[0:nrows, :], ss[0:nrows, :], 1.0 / ncols, EPS, ALU.mult, ALU.add, [ss], [rstd])
    cx.act(rstd[0:nrows, :], rstd[0:nrows, :], AF.Sqrt, [rstd], [rstd])
    cx.recip(rstd[0:nrows, :], rstd[0:nrows, :], [rstd], [rstd])
    cx.stt("dve", out_ap, src_ap, rstd[0:nrows, 0:1], gb[0:nrows, 0:ncols], ALU.mult, ALU.mult,
           list(src_bufs) + [rstd, gb], out_bufs)


def transpose_rows(cx, consts, a_tok, nrows, nk, tp, aT, col0, evac="act"):
    ident = consts["ident"]
    for k in range(nk):
        cx.tr(tp[:, k, 0:nrows], a_tok[0:nrows, k * 128:(k + 1) * 128], ident[0:nrows, 0:nrows],
              [a_tok, ident], [tp])
    cx.cp(evac, aT[:, 0:nk, col0:col0 + nrows], tp[:, 0:nk, 0:nrows], [tp], [aT])


def load_w_cast(cx, dst, src_ap, nk, r=(), kstep=1):
    v = src_ap.rearrange("(k p) f -> p k f", p=128)
    for k in range(0, nk, kstep):
        k1 = min(nk, k + kstep)
        cx.dma("pq", dst[:, k:k1, :], v[:, k:k1, :], w=[dst])


def ffn_pass(cx, h_in, h_out, g_vec, wg, wu, wd, final_g=None, y_out=None):
    P = cx.P
    P.barrier()
    with ExitStack() as st:
        consts = make_consts(cx, st)
        Wg = cx.sb(st, "Wg", [128, 8, DFF], BF16)
        Wu = cx.sb(st, "Wu", [128, 8, DFF], BF16)
        Wd = cx.sb(st, "Wd", [128, NM, D], BF16)
        gb = load_bcast(cx, st, "gffn", g_vec, D)
        gf = load_bcast(cx, st, "gfin", final_g, D) if final_g is not None else None
        vg = wg.rearrange("(k p) f -> p k f", p=128)
        vu = wu.rearrange("(k p) f -> p k f", p=128)
        for k in range(8):
            cx.dma("pq", Wg[:, k, :], vg[:, k, :], w=[Wg])
            cx.dma("pq", Wu[:, k, :], vu[:, k, :], w=[Wu])
        load_w_cast(cx, Wd, wd, NM, kstep=2)
        NRING = 6
        hring = [cx.sb(st, "hblk", [128, D], F32) for _ in range(NRING)]
        a_tok = cx.sb(st, "atok", [128, D], BF16)
        aT = cx.sb(st, "aT", [128, 8, 512], BF16)
        gu = cx.sb(st, "gu", [128, NM, 512], BF16)
        sc = {"junk": cx.sb(st, "junk", [128, D], BF16), "ss": cx.sb(st, "ss", [128, 1], F32),
              "rstd": cx.sb(st, "rstd", [128, 1], F32)}
        sg = cx.sb(st, "sg", [128, 512], F32)
        tp = cx.ps(st, "tp", [128, 8, 128], BF16)
        pg = [cx.ps(st, "pg", [128, 512], F32) for _ in range(2)]
        pu = [cx.ps(st, "pu", [128, 512], F32) for _ in range(2)]
        pd = [cx.ps(st, "pd", [128, 512], F32) for _ in range(2)]
        ring_i = 0
        for (t0, T) in chunk_list():
            nb = (T + 127) // 128
            blks = []
            for tb in range(nb):
                bt = min(128, T - tb * 128)
                hb = hring[ring_i % NRING]
                ring_i += 1
                cx.dma("sp", hb[0:bt, :], h_in[t0 + tb * 128: t0 + tb * 128 + bt, :], w=[hb])
                blks.append((hb, bt))
            for tb, (hb, bt) in enumerate(blks):
                rms_rows(cx, sc, hb[0:bt, :], [hb], bt, D, gb, a_tok[0:bt, :], [a_tok])
                transpose_rows(cx, consts, a_tok, bt, 8, tp, aT, tb * 128)
            for m in range(NM):
                g_ps, u_ps = pg[m % 2], pu[m % 2]
                for k in range(8):
                    cx.mm(g_ps[:, 0:T], Wg[:, k, m * 128:(m + 1) * 128], aT[:, k, 0:T], k == 0, k == 7, [Wg, aT], [g_ps])
                for k in range(8):
                    cx.mm(u_ps[:, 0:T], Wu[:, k, m * 128:(m + 1) * 128], aT[:, k, 0:T], k == 0, k == 7, [Wu, aT], [u_ps])
                cx.act(sg[:, 0:T], g_ps[:, 0:T], AF.Silu, [g_ps], [sg])
                cx.tt("dve", gu[:, m, 0:T], sg[:, 0:T], u_ps[:, 0:T], ALU.mult, [sg, u_ps], [gu])
            for tb, (hb, bt) in enumerate(blks):
                for half in range(2):
                    d_ps = pd[half]
                    for m in range(NM):
                        cx.mm(d_ps[0:bt, :], gu[:, m, tb * 128: tb * 128 + bt], Wd[:, m, half * 512:(half + 1) * 512],
                              m == 0, m == NM - 1, [gu, Wd], [d_ps])
                    cx.tt("dve", hb[0:bt, half * 512:(half + 1) * 512], d_ps[0:bt, :], hb[0:bt, half * 512:(half + 1) * 512],
                          ALU.add, [d_ps, hb], [hb])
                r0 = t0 + tb * 128
                if h_out is not None:
                    cx.dma("pq", h_out[r0:r0 + bt, :], hb[0:bt, :], r=[hb], final=(y_out is None))
                if y_out is not None:
                    rms_rows_f32(cx, sc, hb, bt, gf)
                    cx.dma("pq", y_out[r0:r0 + bt, :], hb[0:bt, :], r=[hb], final=True)


def rms_rows_f32(cx, sc, hb, bt, gf):
    junk, ss, rstd = sc["junk"], sc["ss"], sc["rstd"]
    cx.memset("dve", ss[0:bt, :], 0.0, [ss])
    cx.act(junk[0:bt, :], hb[0:bt, :], AF.Square, [hb, ss], [junk, ss], accum_out=ss[0:bt, :])
    cx.ts("dve", rstd[0:bt, :], ss[0:bt, :], 1.0 / D, EPS, ALU.mult, ALU.add, [ss], [rstd])
    cx.act(rstd[0:bt, :], rstd[0:bt, :], AF.Sqrt, [rstd], [rstd])
    cx.recip(rstd[0:bt, :], rstd[0:bt, :], [rstd], [rstd])
    cx.stt("dve", hb[0:bt, :], hb[0:bt, :], rstd[0:bt, 0:1], gf[0:bt, :], ALU.mult, ALU.mult, [hb, rstd, gf], [hb])


POOL_W = (2, 4, 8, 16)


def pool_pass(cx, h_in, halo_in, h_out, g_vec, pool_w, pool_scale, bands, bhalo):
    P = cx.P
    P.barrier()
    with ExitStack() as st:
        gb = load_bcast(cx, st, "gmix", g_vec, D)
        psb = load_bcast(cx, st, "pscale", pool_scale, D)
        Bd = cx.sb(st, "bands", [128, 12, 128], BF16)
        cx.dma("pq", Bd[:], bands.rearrange("g t p f -> p (g t) f"), w=[Bd])
        Bh = cx.sb(st, "bhalo", [16, 4, 128], BF16)
        cx.dma("pq", Bh[:], bhalo.rearrange("g p f -> p g f"), w=[Bh])
        PW = cx.sb(st, "PW", [128, 8, 256], BF16)
        cx.dma("pq", PW[:], pool_w.rearrange("g (cc p) o -> p (g cc) o", p=128), w=[PW])
        hc = [cx.sb(st, "hc", [128, 4, D], F32) for _ in range(2)]
        hh = [cx.sb(st, "hh", [16, D], F32) for _ in range(2)]
        a_c = cx.sb(st, "a_c", [128, 4, D], BF16)
        a_h = cx.sb(st, "a_h", [16, D], BF16)
        pT = cx.sb(st, "pT", [128, 8, 512], BF16)
        tmp = cx.sb(st, "tmp", [128, 512], F32)
        sc = {"junk": cx.sb(st, "junk", [128, D], BF16), "ss": cx.sb(st, "ss", [128, 1], F32),
              "rstd": cx.sb(st, "rstd", [128, 1], F32)}
        pp = [cx.ps(st, "pp", [128, 512], F32) for _ in range(2)]
        mx = [cx.ps(st, "mx", [128, 512], F32) for _ in range(4)]
        for ci, (t0, T) in enumerate(chunk_list()):
            nb = (T + 127) // 128
            hcc, hhc = hc[ci % 2], hh[ci % 2]
            if T == 512:
                cx.dma("sp", hcc[:], h_in[t0:t0 + 512, :].rearrange("(b p) d -> p b d", p=128), w=[hcc])
            else:
                cx.dma("sp", hcc[0:T, 0, :], h_in[t0:t0 + T, :], w=[hcc])
            cx.dma("sp", hhc[:], halo_in[ci], w=[hhc])
            for tb in range(nb):
                bt = min(128, T - tb * 128)
                rms_rows(cx, sc, hcc[0:bt, tb, :], [hcc], bt, D, gb, a_c[0:bt, tb, :], [a_c])
            rms_rows(cx, sc, hhc[:, :], [hhc], 16, D, gb, a_h[:, :], [a_h])
            for g in range(4):
                for cc in range(2):
                    c0 = g * 256 + cc * 128
                    p_ps = pp[(g * 2 + cc) % 2]
                    for tb in range(nb):
                        bt = min(128, T - tb * 128)
                        bsel = 2 if (ci == 0 and tb == 0) else 0
                        cx.mm(p_ps[:, tb * 128: tb * 128 + bt], a_c[0:bt, tb, c0:c0 + 128], Bd[0:bt, g * 3 + bsel, 0:bt],
                              True, False, [a_c, Bd], [p_ps])
                        if tb == 0:
                            cx.mm(p_ps[:, 0:bt], a_h[0:16, c0:c0 + 128], Bh[0:16, g, 0:bt], False, True, [a_h, Bh], [p_ps])
                        else:
                            cx.mm(p_ps[:, tb * 128: tb * 128 + bt], a_c[:, tb - 1, c0:c0 + 128], Bd[:, g * 3 + 1, 0:bt],
                                  False, True, [a_c, Bd], [p_ps])
                    cx.cp("act", pT[:, g * 2 + cc, 0:T], p_ps[:, 0:T], [p_ps], [pT])
            for tb in range(nb):
                bt = min(128, T - tb * 128)
                for g in range(4):
                    m_ps = mx[(tb % 2) * 2 + g // 2]
                    col = (g % 2) * 256
                    for cc in range(2):
                        cx.mm(m_ps[0:bt, col:col + 256], pT[:, g * 2 + cc, tb * 128: tb * 128 + bt], PW[:, g * 2 + cc, :],
                              cc == 0, cc == 1, [pT, PW], [m_ps])
                for half in range(2):
                    m_ps = mx[(tb % 2) * 2 + half]
                    cx.tt("dve", tmp[0:bt, :], m_ps[0:bt, :], psb[0:bt, half * 512:(half + 1) * 512], ALU.mult, [m_ps, psb], [tmp])
                    cx.tt("dve", hcc[0:bt, tb, half * 512:(half + 1) * 512], tmp[0:bt, :], hcc[0:bt, tb, half * 512:(half + 1) * 512],
                          ALU.add, [tmp, hcc], [hcc])
            if T == 512:
                cx.dma("sp", h_out[t0:t0 + 512, :].rearrange("(b p) d -> p b d", p=128), hcc[:], r=[hcc])
            else:
                cx.dma("sp", h_out[t0:t0 + T, :], hcc[0:T, 0, :], r=[hcc])


def owned_chunks(r):
    return [2 * j + ((j % 2) if r == 0 else 1 - (j % 2)) for j in range(NCH)]


def owned_positions(r):
    pos = []
    for g in owned_chunks(r):
        pos.extend(range(g * 512, (g + 1) * 512))
    pos.extend(range(SEQ + TAIL * r, SEQ + TAIL * (r + 1)))
    return np.array(pos, dtype=np.int64)


def band_tables():
    bands = np.zeros((4, 3, 128, 128), np.float32)
    bhalo = np.zeros((4, 16, 128), np.float32)
    u = np.arange(128)[:, None]
    t = np.arange(128)[None, :]
    for g, w in enumerate(POOL_W):
        cur = ((u <= t) & (u > t - w)).astype(np.float32)
        bands[g, 0] = cur / w - np.eye(128, dtype=np.float32)
        bands[g, 1] = ((u - 128) > (t - w)).astype(np.float32) / w
        cnt = np.minimum(t + 1, w).astype(np.float32)
        bands[g, 2] = cur / cnt - np.eye(128, dtype=np.float32)
        i = np.arange(16)[:, None]
        bhalo[g] = ((i - 16) > (t - w)).astype(np.float32) / w
    return bands, bhalo


def chunk_norm_T(cx, consts, sc, hring, ring_state, h_in, t0, T, gb, a_tok, tp, aT):
    nb = (T + 127) // 128
    for tb in range(nb):
        bt = min(128, T - tb * 128)
        hb = hring[ring_state[0] % len(hring)]
        ring_state[0] += 1
        cx.dma("sp", hb[0:bt, :], h_in[t0 + tb * 128: t0 + tb * 128 + bt, :], w=[hb])
        rms_rows(cx, sc, hb[0:bt, :], [hb], bt, D, gb, a_tok[0:bt, :], [a_tok])
        transpose_rows(cx, consts, a_tok, bt, 8, tp, aT, tb * 128)


def proj_pass_qkv(cx, h_in, g_vec, w_qkv, QT, KT, V, fox=False, b_f=None, lfT=None):
    cx.P.barrier()
    NF = 3088 if fox else 3072
    with ExitStack() as st:
        consts = make_consts(cx, st)
        W = cx.sb(st, "Wqkv", [128, 8, NF], BF16)
        gb = load_bcast(cx, st, "gmix", g_vec, D)
        load_w_cast(cx, W, w_qkv, 8)
        hring = [cx.sb(st, "hblk", [128, D], F32) for _ in range(4)]
        a_tok = cx.sb(st, "atok", [128, D], BF16)
        aT = [cx.sb(st, "aT", [128, 8, 512], BF16) for _ in range(2)]
        sc = {"junk": cx.sb(st, "junk", [128, D], BF16), "ss": cx.sb(st, "ss", [128, 1], F32),
              "rstd": cx.sb(st, "rstd", [128, 1], F32)}
        qst = [cx.sb(st, "qst", [128, 8, 512], BF16) for _ in range(2)]
        kst = [cx.sb(st, "kst", [128, 8, 512], BF16) for _ in range(2)]
        vst = [cx.sb(st, "vst", [128, 4, D], BF16) for _ in range(2)]
        tp = cx.ps(st, "tp", [128, 8, 128], BF16)
        pq = [cx.ps(st, "pq", [128, 512], F32) for _ in range(4)]
        if fox:
            negb = cx.sb(st, "negb", [16, 1], F32)
            cx.dma("sp", negb[:], b_f.rearrange("(p o) -> p o", o=1), w=[negb])
            cx.ts("dve", negb[:], negb[:], -1.0, 0.0, ALU.mult, ALU.add, [negb], [negb])
            one16 = cx.sb(st, "one16", [16, 1], F32)
            cx.memset("pool", one16[:], 1.0, [one16])
            lfs = [cx.sb(st, "lfs", [16, 512], F32) for _ in range(2)]
            pf = cx.ps(st, "pf", [16, 512], F32)
        rs = [0]
        QTv = QT.rearrange("(oc p) t -> p oc t", p=128)
        KTv = KT.rearrange("(oc p) t -> p oc t", p=128)
        for ci, (t0, T) in enumerate(chunk_list()):
            nb = (T + 127) // 128
            aTc = aT[ci % 2]
            chunk_norm_T(cx, consts, sc, hring, rs, h_in, t0, T, gb, a_tok, tp, aTc)
            q_s, k_s, v_s = qst[ci % 2], kst[ci % 2], vst[ci % 2]
            for oc in range(16):
                ps_ = pq[oc % 4]
                for k in range(8):
                    cx.mm(ps_[:, 0:T], W[:, k, oc * 128:(oc + 1) * 128], aTc[:, k, 0:T], k == 0, k == 7, [W, aTc], [ps_])
                if oc < 8:
                    cx.act(q_s[:, oc, 0:T], ps_[:, 0:T], AF.Copy, [ps_], [q_s], scale=0.125)
                else:
                    cx.cp("dve", k_s[:, oc - 8, 0:T], ps_[:, 0:T], [ps_], [k_s])
            cx.dma("pq", QTv[:, :, t0:t0 + T], q_s[:, :, 0:T], r=[q_s], final=True)
            cx.dma("pq", KTv[:, :, t0:t0 + T], k_s[:, :, 0:T], r=[k_s], final=True)
            for tb in range(nb):
                bt = min(128, T - tb * 128)
                for half in range(2):
                    ps_ = pq[(tb * 2 + half) % 4]
                    for k in range(8):
                        cx.mm(ps_[0:bt, :], aTc[:, k, tb * 128: tb * 128 + bt], W[:, k, 2048 + half * 512: 2048 + (half + 1) * 512],
                              k == 0, k == 7, [aTc, W], [ps_])
                    if half == 0:
                        cx.cp("act", v_s[0:bt, tb, 0:512], ps_[0:bt, :], [ps_], [v_s])
                    else:
                        cx.cp("dve", v_s[0:bt, tb, 512:1024], ps_[0:bt, :], [ps_], [v_s])
            if T == 512:
                cx.dma("pq", V[t0:t0 + 512, :].rearrange("(b p) d -> p b d", p=128), v_s[:], r=[v_s], final=True)
            else:
                cx.dma("pq", V[t0:t0 + T, :], v_s[0:T, 0, :], r=[v_s], final=True)
            if fox:
                for k in range(8):
                    cx.mm(pf[:, 0:T], W[:, k, 3072:3088], aTc[:, k, 0:T], k == 0, k == 7, [W, aTc], [pf])
                l_s = lfs[ci % 2]
                cx.act(l_s[:, 0:T], pf[:, 0:T], AF.Exp, [pf, negb], [l_s], scale=-1.0, bias=negb[:, 0:1])
                cx.act(l_s[:, 0:T], l_s[:, 0:T], AF.Ln, [l_s, one16], [l_s], bias=one16[:, 0:1])
                cx.ts("dve", l_s[:, 0:T], l_s[:, 0:T], -1.0, 0.0, ALU.mult, ALU.add, [l_s], [l_s])
                cx.dma("pq", lfT[:, t0:t0 + T], l_s[:, 0:T], r=[l_s], final=True)


def owner(g):
    return 0 if (g % 2) == ((g // 2) % 2) else 1


def attn_pass(cx, kind, Qloc, Kg, Vg, mask, tmask, OT, scale=1.0, Krg=None, lfg=None, sel=None, scr=None, Vgt=None):
    P = cx.P
    P.barrier()
    R = {"sb": 64, "mla": 96, "fox": 70}[kind]
    RQ = 96 if kind == "mla" else 64
    KOFF = 32 if kind == "mla" else 0
    if kind == "fox":
        Fq_d, NFk_d = fox_prep(cx, lfg, sel, scr)
    with ExitStack() as st:
        Kt = [cx.sb(st, "Kt", [R, L], BF16) for _ in range(2)]
        Vt = [cx.sb(st, "Vt", [128, 65, 65], BF16) for _ in range(2)]
        Qt = [cx.sb(st, "Qt", [R, NT], BF16) for _ in range(2)]
        ost = [cx.sb(st, "ost", [64, NT], BF16) for _ in range(2)]
        mk = cx.sb(st, "mk", [128, 16, 512], BF16)
        cx.dma("pq", mk[:], mask.rearrange("a k p q -> p (a k) q"), w=[mk])
        tmk = cx.sb(st, "tmk", [16, 8], BF16)
        cx.dma("pq", tmk[:], tmask, w=[tmk])
        for v in Vt:
            cx.memset("pool", v[:, :, 64:65], 1.0, [v])
        zs = [cx.sb(st, "zs", [128, 512], F32) for _ in range(2)]
        pt = [cx.sb(st, "pt", [128, 512], BF16) for _ in range(3)]
        zps = [cx.ps(st, "zps", [128, 512], F32) for _ in range(2)]
        ops_ = [cx.ps(st, "ops", [128, 512], F32) for _ in range(2)]
        if kind == "sb":
            et = [cx.sb(st, "et", [128, 512], F32) for _ in range(2)]
            spt = [cx.sb(st, "spt", [128, 512], F32) for _ in range(2)]
            t1 = [cx.sb(st, "t1", [128, 512], F32) for _ in range(2)]
            acc = cx.sb(st, "acc", [128, 512], F32)
            U = cx.sb(st, "U", [128, 128], F32)
            ones = cx.sb(st, "ones", [128, 128], F32)
            one1 = cx.sb(st, "one1", [128, 1], F32)
            cx.memset("pool", one1[:], 1.0, [one1])
            cx.memset("pool", ones[:], 1.0, [ones])
            cx.memset("pool", U[:], 1.0, [U])
            cx.P.op("pool", lambda e: e.affine_select(out=U[:], in_=U[:], pattern=[[-1, 128]], compare_op=ALU.is_gt,
                                                      fill=0.0, base=0, channel_multiplier=1), [U.b], [U.b])
            lps = [cx.ps(st, "lps", [128, 512], F32) for _ in range(2)]
        else:
            osb = cx.sb(st, "osb", [65, 512], F32)
            rrow = cx.sb(st, "rrow", [65, 512], F32)
            onesr = cx.sb(st, "onesr", [65, 64], F32)
            cx.memset("pool", onesr[:], 1.0, [onesr])
            bps = cx.ps(st, "bps", [64, 512], F32)
        if kind == "mla":
            for kt in Kt:
                for g in range(16):
                    cx.dma("sp", kt[0:32, g * 512:(g + 1) * 512], Krg[owner(g), :, (g // 2) * 512:(g // 2 + 1) * 512], w=[kt])
                for r in range(2):
                    cx.dma("sp", kt[0:32, SEQ + TAIL * r: SEQ + TAIL * (r + 1)], Krg[r, :, NCH * 512: NT], w=[kt])
        if kind == "fox":
            for kt in Kt:
                cx.memset("pool", kt[64:70, :], 1.0, [kt])
            for qt in Qt:
                cx.memset("pool", qt[64:70, :], 1.0, [qt])
        zi = [0]
        pi = [0]
        oi = [0]

        def load_head(h):
            kt, vt, qt = Kt[h % 2], Vt[h % 2], Qt[h % 2]
            for g in range(16):
                j = g // 2
                hr = (h % 2) * 64
                cx.dma("sp", kt[KOFF:KOFF + 64, g * 512:(g + 1) * 512], Kg[h // 2, owner(g), hr:hr + 64, j * 512:(j + 1) * 512], w=[kt])
                cx.dma("sp", vt[:, 4 * g:4 * g + 4, 0:64],
                       Vg[j // 2, owner(g), (j % 2) * 512:(j % 2) * 512 + 512, h * 64:(h + 1) * 64].rearrange("(b p) d -> p b d", p=128), w=[vt])
            for r in range(2):
                hr = (h % 2) * 64
                cx.dma("sp", kt[KOFF:KOFF + 64, SEQ + TAIL * r: SEQ + TAIL * (r + 1)], Kg[h // 2, r, hr:hr + 64, NCH * 512:NT], w=[kt])
                cx.dma("sp", vt[TAIL * r:TAIL * (r + 1), 64, 0:64], Vgt[r, :, h * 64:(h + 1) * 64], w=[vt])
            cx.dma("sp", qt[0:RQ, :], Qloc[h * RQ:(h + 1) * RQ, :], w=[qt])
            if kind == "fox":
                cx.dma("sp", kt[67:70, :], NFk_d[h], w=[kt])
                cx.dma("sp", qt[64:67, :], Fq_d[h], w=[qt])

        load_head(0)
        for h in range(16):
            if h + 1 < 16:
                load_head(h + 1)
            kt, vt, qt, o_s = Kt[h % 2], Vt[h % 2], Qt[h % 2], ost[h % 2]
            for ci, (t0, T) in enumerate(chunk_list()):
                if T == 512:
                    nkb = 8 * ci + 8
                    blocks = [(kb, 128, (kb - 8 * ci) if kb >= 8 * ci else -1) for kb in range(nkb)]
                    par = ci % 2
                else:
                    blocks = [(kb, 128, -1) for kb in range(64)] + [(64, 16, 0)]
                    par = -1
                o_ps = ops_[oi[0] % 2]
                oi[0] += 1
                if kind == "sb":
                    blocks = blocks[::-1]
                    cx.memset("pool", acc[:, 0:T], 0.0, [acc])
                OR = 64 if kind == "sb" else 65
                for bi, (kb, kn, mi) in enumerate(blocks):
                    first, lastb = bi == 0, bi == len(blocks) - 1
                    z_ps = zps[zi[0] % 2]
                    zi[0] += 1
                    cx.mm(z_ps[0:kn, 0:T], kt[0:R, kb * 128: kb * 128 + kn], qt[0:R, t0:t0 + T], True, True, [kt, qt], [z_ps])
                    src, srcb = z_ps[0:kn, 0:T], z_ps
                    if mi >= 0:
                        z_s = zs[zi[0] % 2]
                        m_ap = mk[:, par * 8 + mi, :] if par >= 0 else tmk[0:16, 0:8]
                        cx.tt("dve", z_s[0:kn, 0:T], z_ps[0:kn, 0:T], m_ap, ALU.add, [z_ps, mk, tmk], [z_s])
                        src, srcb = z_s[0:kn, 0:T], z_s
                    p_t = pt[pi[0] % 3]
                    pi[0] += 1
                    if kind == "sb":
                        e_t, s_t, t_t = et[bi % 2], spt[bi % 2], t1[bi % 2]
                        l_ps = lps[bi % 2]
                        cx.act(e_t[0:kn, 0:T], src, AF.Exp, [srcb], [e_t])
                        cx.act(s_t[0:kn, 0:T], e_t[0:kn, 0:T], AF.Ln, [e_t, one1], [s_t], bias=one1[0:kn, 0:1])
                        cx.mm(l_ps[0:kn, 0:T], U[0:kn, 0:kn], s_t[0:kn, 0:T], True, first, [U, s_t], [l_ps])
                        if not first:
                            cx.mm(l_ps[0:kn, 0:T], ones[:, 0:kn], acc[:, 0:T], False, True, [ones, acc], [l_ps])
                        cx.tt("dve", t_t[0:kn, 0:T], src, s_t[0:kn, 0:T], ALU.subtract, [srcb, s_t], [t_t])
                        cx.tt("dve", t_t[0:kn, 0:T], t_t[0:kn, 0:T], l_ps[0:kn, 0:T], ALU.subtract, [t_t, l_ps], [t_t])
                        cx.act(p_t[0:kn, 0:T], t_t[0:kn, 0:T], AF.Exp, [t_t], [p_t])
                        if not lastb:
                            cx.tt("pool", acc[0:kn, 0:T], acc[0:kn, 0:T], s_t[0:kn, 0:T], ALU.add, [acc, s_t], [acc])
                    else:
                        cx.act(p_t[0:kn, 0:T], src, AF.Exp, [srcb], [p_t], scale=scale)
                    cx.mm(o_ps[0:OR, 0:T], vt[0:kn, kb, 0:OR], p_t[0:kn, 0:T], first, lastb, [vt, p_t], [o_ps])
                if kind == "sb":
                    cx.cp("act", o_s[:, t0:t0 + T], o_ps[0:64, 0:T], [o_ps], [o_s])
                else:
                    cx.cp("act", osb[:, 0:T], o_ps[0:65, 0:T], [o_ps], [osb])
                    cx.recip(rrow[64:65, 0:T], osb[64:65, 0:T], [osb], [rrow])
                    cx.mm(bps[:, 0:T], onesr[64:65, :], rrow[64:65, 0:T], True, True, [onesr, rrow], [bps])
                    cx.tt("dve", o_s[:, t0:t0 + T], osb[0:64, 0:T], bps[:, 0:T], ALU.mult, [osb, bps], [o_s])
            cx.dma("pq", OT[h * 64:(h + 1) * 64, :], o_s[:], r=[o_s], final=True)


def fox_prep(cx, lfg, sel, scr):
    Fq_d, NFk_d = scr["Fq"], scr["NFk"]
    PIECE = 2052
    with ExitStack() as st:
        lf = cx.sb(st, "lf", [16, L], F32)
        F = cx.sb(st, "F", [16, L], F32)
        Fo = cx.sb(st, "Fo", [16, NT], F32)
        onesp = cx.sb(st, "onesp", [16, PIECE], F32)
        zero = cx.sb(st, "zero", [16, 1], F32)
        selt = cx.sb(st, "selt", [16, 18], F32)
        cx.dma("sp", selt[:], sel, w=[selt])
        cx.memset("pool", onesp[:], 1.0, [onesp])
        cx.memset("pool", zero[:], 0.0, [zero])
        for g in range(16):
            cx.dma("sp", lf[:, g * 512:(g + 1) * 512], lfg[owner(g), :, (g // 2) * 512:(g // 2 + 1) * 512], w=[lf])
        for r in range(2):
            cx.dma("sp", lf[:, SEQ + TAIL * r: SEQ + TAIL * (r + 1)], lfg[r, :, NCH * 512:NT], w=[lf])
        for i in range(L // PIECE):
            c0 = i * PIECE
            init = zero[:, 0:1] if i == 0 else F[:, c0 - 1:c0]
            cx.P.op("dve", (lambda c0=c0, init=init: (lambda e: e.tensor_tensor_scan(
                out=F[:, c0:c0 + PIECE], data0=onesp[:], data1=lf[:, c0:c0 + PIECE], initial=init,
                op0=ALU.mult, op1=ALU.add)))(), [lf.b, onesp.b, zero.b, F.b], [F.b])
        for j in range(NCH + 1):
            if j < NCH:
                a0, a1, n, d0 = 2 * j * 512, (2 * j + 1) * 512, 512, j * 512
            else:
                a0, a1, n, d0 = SEQ, SEQ + TAIL, TAIL, NCH * 512
            cx.ts("dve", Fo[:, d0:d0 + n], F[:, a0:a0 + n], selt[:, 2 * j:2 * j + 1], None, ALU.mult, None, [F, selt], [Fo])
            cx.stt("dve", Fo[:, d0:d0 + n], F[:, a1:a1 + n], selt[:, 2 * j + 1:2 * j + 2], Fo[:, d0:d0 + n], ALU.mult, ALU.add,
                   [F, selt, Fo], [Fo])
        hb = [cx.sb(st, "hb", [16, PIECE], BF16) for _ in range(3)]
        h32 = cx.sb(st, "h32", [16, PIECE], F32)
        rr = cx.sb(st, "rr", [16, PIECE], F32)

        def split(src, c0, n, dst, negate):
            s = -1.0 if negate else 1.0
            cx.ts("dve", rr[:, 0:n], src[:, c0:c0 + n], s, 0.0, ALU.mult, ALU.add, [src], [rr])
            for i in range(3):
                cx.cp("dve", hb[i][:, 0:n], rr[:, 0:n], [rr], [hb[i]])
                cx.dma("sp", dst[:, i, c0:c0 + n], hb[i][:, 0:n], r=[hb[i]])
                if i < 2:
                    cx.cp("dve", h32[:, 0:n], hb[i][:, 0:n], [hb[i]], [h32])
                    cx.tt("dve", rr[:, 0:n], rr[:, 0:n], h32[:, 0:n], ALU.subtract, [rr, h32], [rr])

        for i in range(L // PIECE):
            split(F, i * PIECE, PIECE, NFk_d, True)
        for i in range(2):
            split(Fo, i * PIECE, PIECE, Fq_d, False)
    cx.P.barrier()
    return Fq_d, NFk_d


def oproj_pass(cx, h_in, OT, w_o, h_out):
    cx.P.barrier()
    with ExitStack() as st:
        Wo = cx.sb(st, "Wo", [128, 8, D], BF16)
        load_w_cast(cx, Wo, w_o, 8)
        oc = [cx.sb(st, "oc", [128, 8, 512], BF16) for _ in range(2)]
        hc = [cx.sb(st, "hc", [128, 4, D], F32) for _ in range(2)]
        ps_ = [cx.ps(st, "po", [128, 512], F32) for _ in range(4)]
        OTv = OT.rearrange("(k p) t -> p k t", p=128)
        for ci, (t0, T) in enumerate(chunk_list()):
            nb = (T + 127) // 128
            o_c, h_c = oc[ci % 2], hc[ci % 2]
            cx.dma("sp", o_c[:, :, 0:T], OTv[:, :, t0:t0 + T], w=[o_c])
            if T == 512:
                cx.dma("sp", h_c[:], h_in[t0:t0 + 512, :].rearrange("(b p) d -> p b d", p=128), w=[h_c])
            else:
                cx.dma("sp", h_c[0:T, 0, :], h_in[t0:t0 + T, :], w=[h_c])
            for tb in range(nb):
                bt = min(128, T - tb * 128)
                for half in range(2):
                    p_ = ps_[(tb * 2 + half) % 4]
                    for k in range(8):
                        cx.mm(p_[0:bt, :], o_c[:, k, tb * 128: tb * 128 + bt], Wo[:, k, half * 512:(half + 1) * 512],
                              k == 0, k == 7, [o_c, Wo], [p_])
                    cx.tt("dve", h_c[0:bt, tb, half * 512:(half + 1) * 512], p_[0:bt, :], h_c[0:bt, tb, half * 512:(half + 1) * 512],
                          ALU.add, [p_, h_c], [h_c])
            if T == 512:
                cx.dma("pq", h_out[t0:t0 + 512, :].rearrange("(b p) d -> p b d", p=128), h_c[:], r=[h_c])
            else:
                cx.dma("pq", h_out[t0:t0 + T, :], h_c[0:T, 0, :], r=[h_c])


def mask_tables(r, strict):
    m = np.zeros((2, 8, 128, 512), np.float32)
    oc = owned_chunks(r)
    for par in range(2):
        j = par
        qpos = oc[j] * 512 + np.arange(512)[None, :]
        for kbl in range(8):
            kpos = (2 * j) * 512 + kbl * 128 + np.arange(128)[:, None]
            ok = (kpos < qpos) if strict else (kpos <= qpos)
            m[par, kbl] = np.where(ok, 0.0, NEG)
    qpos = SEQ + TAIL * r + np.arange(TAIL)[None, :]
    kpos = SEQ + np.arange(16)[:, None]
    ok = (kpos < qpos) if strict else (kpos <= qpos)
    tm = np.where(ok, 0.0, NEG).astype(np.float32)
    return m, tm


def build_launch1():
    cx = Ctx()
    h_in = cx.din("h_in", [NT, D])
    halo = cx.din("halo", [NCH + 1, 16, D])
    bands = cx.din("bands", [4, 3, 128, 128])
    bhalo = cx.din("bhalo", [4, 16, 128])
    g_mix = cx.din("g_mix", [D])
    g_ffn = cx.din("g_ffn", [D])
    pool_w = cx.din("pool_w", [4, 256, 256])
    pool_scale = cx.din("pool_scale", [D])
    wg = cx.din("wg", [D, DFF])
    wu = cx.din("wu", [D, DFF])
    wd = cx.din("wd", [DFF, D])
    g_mix1 = cx.din("g_mix1", [D])
    w_qkv = cx.din("w_qkv", [D, 3072])
    h_mid = cx.dint("h_mid", [NT, D])
    h_out = cx.dout("h_out", [NT, D])
    QT = cx.dout("QT", [D, NT], BF16)
    KT = cx.dout("KT", [D, NT], BF16)
    V = cx.dout("V", [NT, D], BF16)
    pool_pass(cx, h_in, halo, h_mid, g_mix, pool_w, pool_scale, bands, bhalo)
    ffn_pass(cx, h_mid, h_out, g_ffn, wg, wu, wd)
    proj_pass_qkv(cx, h_out, g_mix1, w_qkv, QT, KT, V)
    return cx.finish()


def proj_pass_mla(cx, h_in, g_vec, w_down, q_norm, kv_norm, w_uq, w_ukv, cs_tm, csT, Qloc, Kn, Kr, V, level=9):
    cx.P.barrier()
    with ExitStack() as st:
        consts = make_consts(cx, st)
        Wdn = cx.sb(st, "Wdn", [128, 8, 672], BF16)
        load_w_cast(cx, Wdn, w_down, 8, kstep=4)
        WuqP = cx.sb(st, "WuqP", [128, 3, 16, 96], BF16)
        WuqS = cx.sb(st, "WuqS", [128, 3, 16, 32], BF16)
        WukN = cx.sb(st, "WukN", [128, 2, 16, 64], BF16)
        WuV = cx.sb(st, "WuV", [128, 2, 16, 64], BF16)
        with ExitStack() as st2:
            Wq_nat = cx.sb(st2, "Wq_nat", [128, 3, 16, 96], BF16)
            Wkv_nat = cx.sb(st2, "Wkv_nat", [128, 2, 16, 2, 64], BF16)
            uqv = w_uq.rearrange("(k p) (h c) -> p k h c", p=128, c=96)
            ukv = w_ukv.rearrange("(k p) (h t c) -> p k h t c", p=128, t=2, c=64)
            for k in range(3):
                cx.dma("pq", Wq_nat[:, k, :, :], uqv[:, k, :, :], w=[Wq_nat])
            for k in range(2):
                cx.dma("pq", Wkv_nat[:, k, :, :, :], ukv[:, k, :, :, :], w=[Wkv_nat])
            for k in range(3 if "W" not in os.environ.get("MLA_SKIP", "") else 0):
                cx.cp("dve", WuqP[:, k, :, 0:32], Wq_nat[:, k, :, 64:96], [Wq_nat], [WuqP])
                cx.cp("pool", WuqP[:, k, :, 32:96], Wq_nat[:, k, :, 0:64], [Wq_nat], [WuqP])
                cx.cp("dve", WuqS[:, k, :, 0:16], Wq_nat[:, k, :, 80:96], [Wq_nat], [WuqS])
                cx.cp("pool", WuqS[:, k, :, 16:32], Wq_nat[:, k, :, 64:80], [Wq_nat], [WuqS])
            for k in range(2 if "W" not in os.environ.get("MLA_SKIP", "") else 0):
                cx.cp("dve", WukN[:, k, :, :], Wkv_nat[:, k, :, 0, :], [Wkv_nat], [WukN])
                cx.cp("pool", WuV[:, k, :, :], Wkv_nat[:, k, :, 1, :], [Wkv_nat], [WuV])
            cx.P.barrier()
        gb = load_bcast(cx, st, "gmix", g_vec, D)
        qnb = load_bcast(cx, st, "qnb", q_norm, 384)
        kvnb = load_bcast(cx, st, "kvnb", kv_norm, 256)
        hring = [cx.sb(st, "hblk", [128, D], F32) for _ in range(4)]
        a_tok = cx.sb(st, "atok", [128, D], BF16)
        aT = [cx.sb(st, "aT", [128, 8, 512], BF16) for _ in range(2)]
        sc = {"junk": cx.sb(st, "junk", [128, D], BF16), "ss": cx.sb(st, "ss", [128, 1], F32),
              "rstd": cx.sb(st, "rstd", [128, 1], F32)}
        m_tok = cx.sb(st, "mtok", [128, 768], BF16)
        cqT = cx.sb(st, "cqT", [128, 3, 512], BF16)
        ckvT = cx.sb(st, "ckvT", [128, 2, 512], BF16)
        krT = [cx.sb(st, "krT", [32, 512], BF16) for _ in range(2)]
        qst = [cx.sb(st, "qst", [96, 16, 512], BF16) for _ in range(2)]
        kst = [cx.sb(st, "kst", [128, 8, 512], BF16) for _ in range(2)]
        vst = [cx.sb(st, "vst", [128, 4, D], BF16) for _ in range(2)]
        cst = [cx.sb(st, "cst", [128, 64], F32) for _ in range(2)]
        csf = [cx.sb(st, "csf", [64, 512], F32) for _ in range(2)]
        snf = [cx.sb(st, "snf", [32, 512], F32) for _ in range(2)]
        kr32 = cx.sb(st, "kr32", [128, 32], F32)
        kr32b = cx.sb(st, "kr32b", [128, 32], F32)
        tA = cx.sb(st, "tA", [32, 512], F32)
        tB = cx.sb(st, "tB", [32, 512], F32)
        tp = cx.ps(st, "tp", [128, 8, 128], BF16)
        dn = [cx.ps(st, "dn", [128, 512], F32) for _ in range(2)]
        pq = [cx.ps(st, "pq", [128, 512], F32) for _ in range(4)]
        rs = [0]
        Qv = Qloc.rearrange("(h c) t -> c h t", c=96)
        Knv = Kn.rearrange("(oc p) t -> p oc t", p=128)
        WukNf = lambda k, oc: WukN[:, k, 2 * oc:2 * oc + 2, :]
        pi = 0
        for ci, (t0, T) in enumerate(chunk_list()):
            nb = (T + 127) // 128
            aTc = aT[ci % 2]
            chunk_norm_T(cx, consts, sc, hring, rs, h_in, t0, T, gb, a_tok, tp, aTc)
            if "C" in os.environ.get("MLA_SKIP", ""):
                cx.dma("pq", Kr[:, t0:t0 + T], aTc[0:32, 0, 0:T], r=[aTc], final=True)
                continue
            cf, sf = csf[ci % 2], snf[ci % 2]
            if "D" not in os.environ.get("MLA_SKIP", ""):
                cx.dma("sp", cf[:, 0:T], csT[:, t0:t0 + T], w=[cf])
                cx.dma("sp", sf[:, 0:T], csT[32:64, t0:t0 + T], w=[sf])
            krTc = krT[ci % 2]
            for tb in range(nb):
                bt = min(128, T - tb * 128)
                ct = cst[tb % 2]
                if "D" not in os.environ.get("MLA_SKIP", ""):
                    cx.dma("sp", ct[0:bt, :], cs_tm[t0 + tb * 128: t0 + tb * 128 + bt, :], w=[ct])
                SKIP = os.environ.get("MLA_SKIP", "")
                for k in range(8 if "M" not in SKIP else 0):
                    cx.mm(dn[0][0:bt, 0:384], aTc[:, k, tb * 128: tb * 128 + bt], Wdn[:, k, 0:384], k == 0, k == 7, [aTc, Wdn], [dn[0]])
                for k in range(8 if "M" not in SKIP else 0):
                    cx.mm(dn[1][0:bt, 0:288], aTc[:, k, tb * 128: tb * 128 + bt], Wdn[:, k, 384:672], k == 0, k == 7, [aTc, Wdn], [dn[1]])
                if "M" in SKIP:
                    cx.cp("dve", m_tok[0:bt, 0:672], a_tok[0:bt, 0:672], [a_tok], [m_tok])
                if "M" in SKIP:
                    pass
                elif "n" in SKIP:
                    cx.cp("dve", m_tok[0:bt, 0:384], dn[0][0:bt, 0:384], [dn[0]], [m_tok])
                    cx.cp("dve", m_tok[0:bt, 384:640], dn[1][0:bt, 0:256], [dn[1]], [m_tok])
                else:
                    rms_rows(cx, sc, dn[0][0:bt, 0:384], [dn[0]], bt, 384, qnb, m_tok[0:bt, 0:384], [m_tok])
                    rms_rows(cx, sc, dn[1][0:bt, 0:256], [dn[1]], bt, 256, kvnb, m_tok[0:bt, 384:640], [m_tok])
                if "M" in SKIP:
                    pass
                elif "r" in SKIP:
                    cx.cp("dve", m_tok[0:bt, 640:672], dn[1][0:bt, 256:288], [dn[1]], [m_tok])
                else:
                    cx.tt("dve", kr32[0:bt, :], dn[1][0:bt, 256:288], ct[0:bt, 0:32], ALU.mult, [dn[1], ct], [kr32])
                    cx.tt("dve", kr32b[0:bt, 0:16], dn[1][0:bt, 272:288], ct[0:bt, 32:48], ALU.mult, [dn[1], ct], [kr32b])
                    cx.tt("dve", kr32b[0:bt, 16:32], dn[1][0:bt, 256:272], ct[0:bt, 48:64], ALU.mult, [dn[1], ct], [kr32b])
                    cx.tt("dve", m_tok[0:bt, 640:672], kr32[0:bt, :], kr32b[0:bt, :], ALU.add, [kr32, kr32b], [m_tok])
                ident = consts["ident"]
                if "T" in SKIP:
                    cx.cp("act", krTc[:, tb * 128: tb * 128 + bt], aTc[0:32, 0, tb * 128: tb * 128 + bt], [aTc, m_tok], [krTc])
                    continue
                for k in range(5):
                    cx.tr(tp[:, k, 0:bt], m_tok[0:bt, k * 128:(k + 1) * 128], ident[0:bt, 0:bt], [m_tok, ident], [tp])
                if "t" in SKIP:
                    cx.tr(tp[:, 5, 0:bt], m_tok[0:bt, 640:768], ident[0:bt, 0:bt], [m_tok, ident], [tp])
                else:
                    cx.tr(tp[0:32, 5, 0:bt], m_tok[0:bt, 640:672], ident[0:bt, 0:bt], [m_tok, ident], [tp])
                cx.cp("act", cqT[:, :, tb * 128: tb * 128 + bt], tp[:, 0:3, 0:bt], [tp], [cqT])
                cx.cp("act", ckvT[:, :, tb * 128: tb * 128 + bt], tp[:, 3:5, 0:bt], [tp], [ckvT])
                cx.cp("act", krTc[:, tb * 128: tb * 128 + bt], tp[0:32, 5, 0:bt], [tp], [krTc])
            cx.dma("pq", Kr[:, t0:t0 + T], krTc[:, 0:T], r=[krTc], final=True)
            q_s, k_s, v_s = qst[ci % 2], kst[ci % 2], vst[ci % 2]
            if level < 2:
                continue
            for h in range(16):
                ph, psw = pq[pi % 4], pq[(pi + 1) % 4]
                pi += 2
                for k in range(3):
                    cx.mm(ph[0:96, 0:T], WuqP[:, k, h, :], cqT[:, k, 0:T], k == 0, k == 2, [WuqP, cqT], [ph])
                for k in range(3):
                    cx.mm(psw[0:32, 0:T], WuqS[:, k, h, :], cqT[:, k, 0:T], k == 0, k == 2, [WuqS, cqT], [psw])
                cx.tt("dve", tA[:, 0:T], ph[0:32, 0:T], cf[0:32, 0:T], ALU.mult, [ph, cf], [tA])
                cx.tt("dve", tB[:, 0:T], psw[0:32, 0:T], sf[:, 0:T], ALU.mult, [psw, sf], [tB])
                cx.tt("pool", q_s[0:32, h, 0:T], tA[:, 0:T], tB[:, 0:T], ALU.add, [tA, tB], [q_s])
                cx.cp("act", q_s[32:64, h, 0:T], ph[32:64, 0:T], [ph], [q_s])
                cx.cp("act", q_s[64:96, h, 0:T], ph[64:96, 0:T], [ph], [q_s])
            cx.dma("pq", Qv[:, :, t0:t0 + T], q_s[:, :, 0:T], r=[q_s], final=True)
            if level < 3:
                continue
            for oc in range(8):
                ps_ = pq[pi % 4]
                pi += 1
                for k in range(2):
                    cx.mm(ps_[:, 0:T], WukN[:, k, 2 * oc:2 * oc + 2, :], ckvT[:, k, 0:T], k == 0, k == 1, [WukN, ckvT], [ps_])
                if oc % 2 == 0:
                    cx.cp("act", k_s[:, oc, 0:T], ps_[:, 0:T], [ps_], [k_s])
                else:
                    cx.cp("dve", k_s[:, oc, 0:T], ps_[:, 0:T], [ps_], [k_s])
            cx.dma("pq", Knv[:, :, t0:t0 + T], k_s[:, :, 0:T], r=[k_s], final=True)
            if level < 4:
                continue
            for tb in range(nb):
                bt = min(128, T - tb * 128)
                for half in range(2):
                    ps_ = pq[pi % 4]
                    pi += 1
                    for k in range(2):
                        cx.mm(ps_[0:bt, :], ckvT[:, k, tb * 128: tb * 128 + bt], WuV[:, k, half * 8:(half + 1) * 8, :],
                              k == 0, k == 1, [ckvT, WuV], [ps_])
                    if half == 0:
                        cx.cp("act", v_s[0:bt, tb, 0:512], ps_[0:bt, :], [ps_], [v_s])
                    else:
                        cx.cp("dve", v_s[0:bt, tb, 512:1024], ps_[0:bt, :], [ps_], [v_s])
            if T == 512:
                cx.dma("pq", V[t0:t0 + 512, :].rearrange("(b p) d -> p b d", p=128), v_s[:], r=[v_s], final=True)
            else:
                cx.dma("pq", V[t0:t0 + T, :], v_s[0:T, 0, :], r=[v_s], final=True)


def rope_tables(r):
    pos = owned_positions(r).astype(np.float32)
    inv = (np.float32(10000.0) ** (-np.arange(0, 32, 2, dtype=np.float32) / np.float32(32))).astype(np.float32)
    ang = (pos[:, None] * inv[None, :]).astype(np.float32)
    c, s = np.cos(ang).astype(np.float32), np.sin(ang).astype(np.float32)
    tm = np.concatenate([c, c, -s, s], axis=1).astype(np.float32)
    return np.ascontiguousarray(tm), np.ascontiguousarray(tm.T)


def _ffn_inputs(cx):
    return (cx.din("g_ffn", [D]), cx.din("wg", [D, DFF]), cx.din("wu", [D, DFF]), cx.din("wd", [DFF, D]))


def _attn_inputs(cx, qrows):
    return dict(h_in=cx.din("h_in", [NT, D]), QT=cx.din("QT", [qrows, NT], BF16), Kg=cx.din("Kg", [2, D, NT], BF16),
                Vg=cx.din("Vg", [2, NT, D], BF16), mask=cx.din("mask", [2, 8, 128, 512]), tmask=cx.din("tmask", [16, 8]),
                w_o=cx.din("w_o", [D, D]))


def build_launch2():
    cx = Ctx()
    a = _attn_inputs(cx, D)
    g_ffn, wg, wu, wd = _ffn_inputs(cx)
    g_mix = cx.din("g_mix", [D])
    w_down = cx.din("w_down", [D, 672])
    q_norm = cx.din("q_norm", [384])
    kv_norm = cx.din("kv_norm", [256])
    w_uq = cx.din("w_uq", [384, 1536])
    w_ukv = cx.din("w_ukv", [256, 2048])
    cs_tm = cx.din("cs_tm", [NT, 64])
    csT = cx.din("csT", [64, NT])
    OT = cx.dint("OT", [D, NT], BF16)
    h_mid = cx.dint("h_mid", [NT, D])
    h_out = cx.dout("h_out", [NT, D])
    Qn = cx.dout("Qn", [16 * 96, NT], BF16)
    Kn = cx.dout("Kn", [D, NT], BF16)
    Kr = cx.dout("Kr", [32, NT], BF16)
    Vn = cx.dout("Vn", [NT, D], BF16)
    attn_pass(cx, "sb", a["QT"], a["Kg"], a["Vg"], a["mask"], a["tmask"], OT)
    oproj_pass(cx, a["h_in"], OT, a["w_o"], h_mid)
    ffn_pass(cx, h_mid, h_out, g_ffn, wg, wu, wd)
    proj_pass_mla(cx, h_out, g_mix, w_down, q_norm, kv_norm, w_uq, w_ukv, cs_tm, csT, Qn, Kn, Kr, Vn)
    return cx.finish()


def build_launch3():
    cx = Ctx()
    a = _attn_inputs(cx, 16 * 96)
    Krg = cx.din("Krg", [2, 32, NT], BF16)
    g_ffn, wg, wu, wd = _ffn_inputs(cx)
    g_mix = cx.din("g_mix", [D])
    w_qkvf = cx.din("w_qkvf", [D, 3088])
    b_f = cx.din("b_f", [16])
    OT = cx.dint("OT", [D, NT], BF16)
    h_mid = cx.dint("h_mid", [NT, D])
    h_out = cx.dout("h_out", [NT, D])
    Qn = cx.dout("Qn", [D, NT], BF16)
    Kn = cx.dout("Kn", [D, NT], BF16)
    Vn = cx.dout("Vn", [NT, D], BF16)
    lfT = cx.dout("lfT", [16, NT], F32)
    attn_pass(cx, "mla", a["QT"], a["Kg"], a["Vg"], a["mask"], a["tmask"], OT, scale=float(96 ** -0.5), Krg=Krg)
    oproj_pass(cx, a["h_in"], OT, a["w_o"], h_mid)
    ffn_pass(cx, h_mid, h_out, g_ffn, wg, wu, wd)
    proj_pass_qkv(cx, h_out, g_mix, w_qkvf, Qn, Kn, Vn, fox=True, b_f=b_f, lfT=lfT)
    return cx.finish()


def build_launch4():
    cx = Ctx()
    a = _attn_inputs(cx, D)
    lfg = cx.din("lfg", [2, 16, NT], F32)
    sel = cx.din("sel", [16, 18], F32)
    g_ffn, wg, wu, wd = _ffn_inputs(cx)
    g_fin = cx.din("g_fin", [D])
    OT = cx.dint("OT", [D, NT], BF16)
    h_mid = cx.dint("h_mid", [NT, D])
    scr = {"Fq": cx.dint("Fq", [16, 3, NT], BF16), "NFk": cx.dint("NFk", [16, 3, L], BF16)}
    y = cx.dout("y", [NT, D])
    attn_pass(cx, "fox", a["QT"], a["Kg"], a["Vg"], a["mask"], a["tmask"], OT, lfg=lfg, sel=sel, scr=scr)
    oproj_pass(cx, a["h_in"], OT, a["w_o"], h_mid)
    ffn_pass(cx, h_mid, None, g_ffn, wg, wu, wd, final_g=g_fin, y_out=y)
    return cx.finish()


_NC_CACHE = {}


def _get_nc(name, fn):
    if name not in _NC_CACHE:
        _NC_CACHE[name] = fn()
    return _NC_CACHE[name]


def _run(nc, in_maps, tag=""):
    res = run_bass_kernel_spmd(nc, in_maps, core_ids=list(range(8)))
    dbg = os.environ.get("MK_DBG")
    if dbg:
        os.makedirs(dbg, exist_ok=True)
        for k in res.results[0]:
            a = np.stack([np.asarray(res.results[c][k]) for c in range(2)])
            if a.dtype != np.float32:
                a = a.view(np.uint16)
            np.save(os.path.join(dbg, "%s_%s.npy" % (tag, k)), a)
    return res.results


def _pair(arrs, c):
    b = c // 2
    return np.stack([np.asarray(arrs[2 * b]), np.asarray(arrs[2 * b + 1])])


def kernel_unfused(x, meta, norm_mix, norm_ffn, pool_w, pool_scale, sb_w_qkv, sb_w_o,
           mla_w_down, mla_q_norm, mla_kv_norm, mla_w_uq, mla_w_ukv, mla_w_o,
           fox_w_qkvf, fox_b_f, fox_w_o, ffn_w_gate, ffn_w_up, ffn_w_down, final_norm):
    f = lambda a: np.ascontiguousarray(np.asarray(a, dtype=np.float32))
    x, meta = f(x), f(meta)
    norm_mix, norm_ffn = f(norm_mix), f(norm_ffn)
    wg, wu, wd = f(ffn_w_gate), f(ffn_w_up), f(ffn_w_down)
    bands, bhalo = band_tables()
    cores = list(range(8))
    in_maps = []
    for c in cores:
        b, r = c // 2, c % 2
        hfull = np.concatenate([meta, x[b]], axis=0)
        pos = owned_positions(r)
        starts = [g * 512 for g in owned_chunks(r)] + [SEQ + TAIL * r]
        halo = np.zeros((NCH + 1, 16, D), np.float32)
        for i, s0 in enumerate(starts):
            lo = max(0, s0 - 16)
            if s0 > 0:
                halo[i, 16 - (s0 - lo):] = hfull[lo:s0]
        bd = bands.copy()
        if r == 1:
            bd[:, 2] = bd[:, 0]
        in_maps.append({"h_in": np.ascontiguousarray(hfull[pos]), "halo": halo, "bands": bd, "bhalo": bhalo,
                        "g_mix": norm_mix[0], "g_ffn": norm_ffn[0], "pool_w": f(pool_w)[0], "pool_scale": f(pool_scale)[0],
                        "wg": wg[0], "wu": wu[0], "wd": wd[0], "g_mix1": norm_mix[1], "w_qkv": f(sb_w_qkv)[0]})
    r1 = _run(_get_nc("l1", build_launch1), in_maps, "l1")
    QT = [r1[c]["QT"] for c in cores]
    KT = [r1[c]["KT"] for c in cores]
    VV = [r1[c]["V"] for c in cores]
    in_maps = []
    for c in cores:
        r = c % 2
        m, tm = mask_tables(r, strict=True)
        cs_tm, csT = rope_tables(r)
        in_maps.append({"h_in": r1[c]["h_out"], "QT": QT[c], "Kg": _pair(KT, c), "Vg": _pair(VV, c), "mask": m, "tmask": tm,
                        "w_o": f(sb_w_o)[0], "g_ffn": norm_ffn[1], "wg": wg[1], "wu": wu[1], "wd": wd[1],
                        "g_mix": norm_mix[2], "w_down": f(mla_w_down)[0], "q_norm": f(mla_q_norm)[0], "kv_norm": f(mla_kv_norm)[0],
                        "w_uq": f(mla_w_uq)[0], "w_ukv": f(mla_w_ukv)[0], "cs_tm": cs_tm, "csT": csT})
    r2 = _run(_get_nc("l2", build_launch2), in_maps, "l2")
    del r1
    Qn = [r2[c]["Qn"] for c in cores]
    Kn = [r2[c]["Kn"] for c in cores]
    Kr = [r2[c]["Kr"] for c in cores]
    Vn = [r2[c]["Vn"] for c in cores]
    in_maps = []
    for c in cores:
        r = c % 2
        m, tm = mask_tables(r, strict=False)
        in_maps.append({"h_in": r2[c]["h_out"], "QT": Qn[c], "Kg": _pair(Kn, c), "Vg": _pair(Vn, c), "Krg": _pair(Kr, c),
                        "mask": m, "tmask": tm, "w_o": f(mla_w_o)[0], "g_ffn": norm_ffn[2], "wg": wg[2], "wu": wu[2], "wd": wd[2],
                        "g_mix": norm_mix[3], "w_qkvf": f(fox_w_qkvf)[0], "b_f": f(fox_b_f)[0]})
    r3 = _run(_get_nc("l3", build_launch3), in_maps, "l3")
    del r2
    Qn = [r3[c]["Qn"] for c in cores]
    Kn = [r3[c]["Kn"] for c in cores]
    Vn = [r3[c]["Vn"] for c in cores]
    lf = [r3[c]["lfT"] for c in cores]
    in_maps = []
    for c in cores:
        r = c % 2
        m, tm = mask_tables(r, strict=False)
        sel = np.zeros((16, 18), np.float32)
        oc = owned_chunks(r)
        for j in range(NCH):
            sel[:, 2 * j + (oc[j] - 2 * j)] = 1.0
        sel[:, 16 + r] = 1.0
        in_maps.append({"h_in": r3[c]["h_out"], "QT": Qn[c], "Kg": _pair(Kn, c), "Vg": _pair(Vn, c), "lfg": _pair(lf, c),
                        "sel": sel, "mask": m, "tmask": tm, "w_o": f(fox_w_o)[0], "g_ffn": norm_ffn[3],
                        "wg": wg[3], "wu": wu[3], "wd": wd[3], "g_fin": f(final_norm)})
    r4 = _run(_get_nc("l4", build_launch4), in_maps, "l4")
    del r3
    out = np.zeros((4, SEQ, D), np.float32)
    for c in cores:
        b, r = c // 2, c % 2
        pos = owned_positions(r)
        y = np.asarray(r4[c]["y"])
        keep = pos >= NMETA
        out[b, pos[keep] - NMETA] = y[keep]
    return out


PAIRS = [[0, 1], [2, 3], [4, 5], [6, 7]]


def all_gather(cx, pairs):
    cx.P.barrier()
    for (src, dst) in pairs:
        cx.P.op("cc", (lambda src=src, dst=dst: (lambda e: e.collective_compute(
            "AllGather", ALU.bypass, replica_groups=PAIRS, ins=[src.opt()], outs=[dst.opt()])))(), [], [])
    cx.P.barrier()


def build_fused():
    cx = Ctx()
    h_in = cx.din("h_in", [NT, D])
    halo = cx.din("halo", [NCH + 1, 16, D])
    bands = cx.din("bands", [4, 3, 128, 128])
    bhalo = cx.din("bhalo", [4, 16, 128])
    norm_mix = cx.din("norm_mix", [4, D])
    norm_ffn = cx.din("norm_ffn", [4, D])
    pool_w = cx.din("pool_w", [4, 256, 256])
    pool_scale = cx.din("pool_scale", [D])
    wg = cx.din("wg", [4, D, DFF])
    wu = cx.din("wu", [4, D, DFF])
    wd = cx.din("wd", [4, DFF, D])
    sb_w_qkv = cx.din("sb_w_qkv", [D, 3072])
    sb_w_o = cx.din("sb_w_o", [D, D])
    w_down = cx.din("mla_w_down", [D, 672])
    q_norm = cx.din("mla_q_norm", [384])
    kv_norm = cx.din("mla_kv_norm", [256])
    w_uq = cx.din("mla_w_uq", [384, 1536])
    w_ukv = cx.din("mla_w_ukv", [256, 2048])
    mla_w_o = cx.din("mla_w_o", [D, D])
    fox_w_qkvf = cx.din("fox_w_qkvf", [D, 3088])
    fox_b_f = cx.din("fox_b_f", [16])
    fox_w_o = cx.din("fox_w_o", [D, D])
    g_fin = cx.din("g_fin", [D])
    mask_lt = cx.din("mask_lt", [2, 8, 128, 512])
    tmask_lt = cx.din("tmask_lt", [16, 8])
    mask_le = cx.din("mask_le", [2, 8, 128, 512])
    tmask_le = cx.din("tmask_le", [16, 8])
    cs_tm = cx.din("cs_tm", [NT, 64])
    csT = cx.din("csT", [64, NT])
    sel = cx.din("sel", [16, 18])
    y = cx.dout("y", [NT, D])
    hA = cx.dint("hA", [NT, D])
    hB = cx.dint("hB", [NT, D])
    h_mid = cx.dint("h_mid", [NT, D])
    Ql = cx.dint("Ql", [16 * 96, NT], BF16)
    Kl = cx.dint("Kl", [D, NT], BF16)
    Vl = cx.dint("Vl", [NT, D], BF16)
    Krl = cx.dint("Krl", [32, NT], BF16)
    lfl = cx.dint("lfl", [16, NT], F32)
    Kg = cx.dint("Kg", [8, 2, 128, NT], BF16)
    Vg = cx.dint("Vg", [4, 2, 1024, D], BF16)
    Vgt = cx.dint("Vgt", [2, TAIL, D], BF16)
    Krg2 = cx.dint("Krg2", [64, NT], BF16)
    lfg2 = cx.dint("lfg2", [32, NT], F32)
    OT = cx.dint("OT", [D, NT], BF16)
    scr = {"Fq": cx.dint("Fq", [16, 3, NT], BF16), "NFk": cx.dint("NFk", [16, 3, L], BF16)}
    kv_pairs = [(Kl[p * 128:(p + 1) * 128, :], Kg[p].rearrange("r d t -> (r d) t")) for p in range(8)]
    kv_pairs += [(Vl[p * 1024:(p + 1) * 1024, :], Vg[p].rearrange("r n d -> (r n) d")) for p in range(4)]
    kv_pairs += [(Vl[NCH * 512:NT, :], Vgt.rearrange("r n d -> (r n) d"))]
    Krg = Krg2.rearrange("(r d) t -> r d t", r=2)
    lfg = lfg2.rearrange("(r d) t -> r d t", r=2)
    Q64 = Ql[0:D, :]
    pool_pass(cx, h_in, halo, h_mid, norm_mix[0], pool_w, pool_scale, bands, bhalo)
    ffn_pass(cx, h_mid, hA, norm_ffn[0], wg[0], wu[0], wd[0])
    proj_pass_qkv(cx, hA, norm_mix[1], sb_w_qkv, Q64, Kl, Vl)
    all_gather(cx, kv_pairs)
    attn_pass(cx, "sb", Q64, Kg, Vg, mask_lt, tmask_lt, OT, Vgt=Vgt)
    oproj_pass(cx, hA, OT, sb_w_o, h_mid)
    ffn_pass(cx, h_mid, hB, norm_ffn[1], wg[1], wu[1], wd[1])
    proj_pass_mla(cx, hB, norm_mix[2], w_down, q_norm, kv_norm, w_uq, w_ukv, cs_tm, csT, Ql, Kl, Krl, Vl)
    all_gather(cx, kv_pairs + [(Krl, Krg2)])
    attn_pass(cx, "mla", Ql, Kg, Vg, mask_le, tmask_le, OT, scale=float(96 ** -0.5), Krg=Krg, Vgt=Vgt)
    oproj_pass(cx, hB, OT, mla_w_o, h_mid)
    ffn_pass(cx, h_mid, hA, norm_ffn[2], wg[2], wu[2], wd[2])
    proj_pass_qkv(cx, hA, norm_mix[3], fox_w_qkvf, Q64, Kl, Vl, fox=True, b_f=fox_b_f, lfT=lfl)
    all_gather(cx, kv_pairs + [(lfl, lfg2)])
    attn_pass(cx, "fox", Q64, Kg, Vg, mask_le, tmask_le, OT, lfg=lfg, sel=sel, scr=scr, Vgt=Vgt)
    oproj_pass(cx, hA, OT, fox_w_o, h_mid)
    ffn_pass(cx, h_mid, None, norm_ffn[3], wg[3], wu[3], wd[3], final_g=g_fin, y_out=y)
    return cx.finish()


def kernel(x, meta, norm_mix, norm_ffn, pool_w, pool_scale, sb_w_qkv, sb_w_o,
           mla_w_down, mla_q_norm, mla_kv_norm, mla_w_uq, mla_w_ukv, mla_w_o,
           fox_w_qkvf, fox_b_f, fox_w_o, ffn_w_gate, ffn_w_up, ffn_w_down, final_norm):
    f = lambda a: np.ascontiguousarray(np.asarray(a, dtype=np.float32))
    x, meta = f(x), f(meta)
    bands, bhalo = band_tables()
    shared = {"norm_mix": f(norm_mix), "norm_ffn": f(norm_ffn), "pool_w": f(pool_w)[0], "pool_scale": f(pool_scale)[0],
              "wg": f(ffn_w_gate), "wu": f(ffn_w_up), "wd": f(ffn_w_down), "sb_w_qkv": f(sb_w_qkv)[0], "sb_w_o": f(sb_w_o)[0],
              "mla_w_down": f(mla_w_down)[0], "mla_q_norm": f(mla_q_norm)[0], "mla_kv_norm": f(mla_kv_norm)[0],
              "mla_w_uq": f(mla_w_uq)[0], "mla_w_ukv": f(mla_w_ukv)[0], "mla_w_o": f(mla_w_o)[0],
              "fox_w_qkvf": f(fox_w_qkvf)[0], "fox_b_f": f(fox_b_f)[0], "fox_w_o": f(fox_w_o)[0], "g_fin": f(final_norm),
              "bhalo": bhalo}
    per_rank = []
    for r in range(2):
        m_lt, t_lt = mask_tables(r, strict=True)
        m_le, t_le = mask_tables(r, strict=False)
        cs_tm, csT = rope_tables(r)
        sel = np.zeros((16, 18), np.float32)
        oc = owned_chunks(r)
        for j in range(NCH):
            sel[:, 2 * j + (oc[j] - 2 * j)] = 1.0
        sel[:, 16 + r] = 1.0
        bd = bands.copy()
        if r == 1:
            bd[:, 2] = bd[:, 0]
        per_rank.append({"mask_lt": m_lt, "tmask_lt": t_lt, "mask_le": m_le, "tmask_le": t_le, "cs_tm": cs_tm, "csT": csT,
                         "sel": sel, "bands": bd})
    in_maps = []
    for c in range(8):
        b, r = c // 2, c % 2
        hfull = np.concatenate([meta, x[b]], axis=0)
        pos = owned_positions(r)
        starts = [g * 512 for g in owned_chunks(r)] + [SEQ + TAIL * r]
        halo = np.zeros((NCH + 1, 16, D), np.float32)
        for i, s0 in enumerate(starts):
            lo = max(0, s0 - 16)
            if s0 > 0:
                halo[i, 16 - (s0 - lo):] = hfull[lo:s0]
        m = {"h_in": np.ascontiguousarray(hfull[pos]), "halo": halo}
        m.update(shared)
        m.update(per_rank[r])
        in_maps.append(m)
    res = _run(_get_nc("fused", build_fused), in_maps, "fused")
    out = np.zeros((4, SEQ, D), np.float32)
    for c in range(8):
        b, r = c // 2, c % 2
        pos = owned_positions(r)
        yv = np.asarray(res[c]["y"])
        keep = pos >= NMETA
        out[b, pos[keep] - NMETA] = yv[keep]
    return out
```

```python
import os
import numpy as np
from contextlib import ExitStack
import ml_dtypes
import concourse.bass as bass
import concourse.mybir as mybir
from concourse.bass_utils import run_bass_kernel_spmd

F32 = mybir.dt.float32
BF16 = mybir.dt.bfloat16
AF = mybir.ActivationFunctionType
ALU = mybir.AluOpType

D = 1024
DFF = 2816
NM = DFF // 128
SEQ = 8192
NMETA = 16
L = SEQ + NMETA
NT = L // 2
NCH = 8
TAIL = 8
EPS = 1e-6
NEG = -30000.0


class Buf:
    __slots__ = ("name", "w", "r")

    def __init__(self, name=""):
        self.name = name
        self.w = None
        self.r = []


class Op:
    __slots__ = ("eng", "fn", "deps", "need_inc", "val", "sem", "is_dma", "idx", "epoch")


COMPUTE = ("pe", "act", "dve", "pool")
QUEUES = ("sp", "pq", "cc")
QINC = {"sp": 16, "pq": 16, "cc": 1}


def _stream(eng):
    return "pool" if eng in ("pq", "cc") else eng


class Prog:
    def __init__(self, nc, n_dma_sems=8):
        self.nc = nc
        self.ops = []
        self.n_dma_sems = n_dma_sems
        self.last = {}
        self.dma_since = []
        self.pending = {}
        self.epoch = 0
        self.ep_cnt = {}

    def op(self, eng, fn, reads=(), writes=()):
        o = Op()
        o.eng = eng
        o.fn = fn
        o.is_dma = eng in QUEUES
        o.need_inc = o.is_dma
        o.val = None
        o.sem = None
        o.idx = len(self.ops)
        o.epoch = self.epoch
        self.ep_cnt[eng] = self.ep_cnt.get(eng, 0) + 1
        deps = []
        for b in reads:
            if b.w is not None:
                deps.append(b.w)
        for b in writes:
            if b.w is not None:
                deps.append(b.w)
            deps.extend(b.r)
        st = _stream(eng)
        if st in self.pending:
            deps.extend(self.pending.pop(st))
        for b in reads:
            b.r = [x for x in b.r if x.is_dma or _stream(x.eng) != st] + [o]
        for b in writes:
            b.w = o
            b.r = []
        seen = set()
        dd = []
        for d in deps:
            if d.idx not in seen:
                seen.add(d.idx)
                dd.append(d)
        o.deps = dd
        self.ops.append(o)
        self.last[st] = o
        if o.is_dma:
            self.dma_since.append(o)
        return o

    def barrier(self):
        if self.ep_cnt and max(self.ep_cnt.values()) > 20000:
            self.epoch += 1
            self.ep_cnt = {}
        deps = list(self.last.values()) + list(self.dma_since)
        self.dma_since = []
        for st in ("pe", "act", "dve", "pool", "sp"):
            self.pending[st] = list(self.pending.get(st, [])) + deps

    def emit(self, final_wait_ops=()):
        nc = self.nc
        ops = self.ops
        for o in ops:
            so = _stream(o.eng)
            for d in o.deps:
                if d.is_dma:
                    continue
                if _stream(d.eng) == so and so == "pe":
                    continue
                d.need_inc = True
        with ExitStack() as stack:
            sems = {(e, ep): stack.enter_context(nc.semaphore("s_%s%d" % (e, ep))) for e in COMPUTE for ep in range(self.epoch + 1)}
            nds = {q: (1 if q == "cc" else self.n_dma_sems) for q in QUEUES}
            dsems = {q: [stack.enter_context(nc.semaphore("d_%s%d" % (q, i))) for i in range(nds[q])]
                     for q in QUEUES}
            cnt = {k: 0 for k in sems}
            dcnt = {q: [0] * nds[q] for q in QUEUES}
            drr = {q: 0 for q in QUEUES}
            per = {s: [] for s in ("pe", "act", "dve", "pool", "sp")}
            for o in ops:
                if o.is_dma:
                    i = drr[o.eng]
                    drr[o.eng] = (i + 1) % nds[o.eng]
                    dcnt[o.eng][i] += QINC[o.eng]
                    o.sem = dsems[o.eng][i]
                    o.val = dcnt[o.eng][i]
                elif o.need_inc:
                    cnt[(o.eng, o.epoch)] += 1
                    o.sem = sems[(o.eng, o.epoch)]
                    o.val = cnt[(o.eng, o.epoch)]
                per[_stream(o.eng)].append(o)
            block = stack.enter_context(nc.Block())

            def run(sname, eng_obj):
                waited = {}
                for o in per[sname]:
                    ws = []
                    for d in o.deps:
                        if (not d.is_dma) and _stream(d.eng) == sname and sname == "pe":
                            continue
                        ws.append((d.sem, d.val))
                    if o.is_dma and o.val > QINC[o.eng]:
                        ws.append((o.sem, o.val - QINC[o.eng]))
                    for (s, v) in ws:
                        k = id(s)
                        if waited.get(k, 0) >= v:
                            continue
                        waited[k] = v
                        eng_obj.wait_ge(s, v)
                    ins = o.fn(eng_obj)
                    if o.need_inc:
                        ins.then_inc(o.sem, QINC[o.eng] if o.is_dma else 1)
                if sname == "sp":
                    fin = {}
                    for o in final_wait_ops:
                        if o.val > fin.get(id(o.sem), (None, 0))[1]:
                            fin[id(o.sem)] = (o.sem, o.val)
                    for (s, v) in fin.values():
                        eng_obj.wait_ge(s, v)

            @block.tensor
            def _(e):
                run("pe", e)

            @block.scalar
            def _(e):
                run("act", e)

            @block.vector
            def _(e):
                run("dve", e)

            @block.gpsimd
            def _(e):
                run("pool", e)

            @block.sync
            def _(e):
                run("sp", e)
        return nc


class Tn:
    __slots__ = ("t", "b")

    def __init__(self, t, name=""):
        self.t = t
        self.b = Buf(name)

    def __getitem__(self, k):
        return self.t[k]


def _bufs(lst):
    return [x.b if isinstance(x, Tn) else x for x in lst]


class Ctx:
    def __init__(self):
        self.nc = bass.Bass("TRN2", target_bir_lowering=False)
        self.P = Prog(self.nc)
        self.outs = []
        self.uid = 0
        self.ext_in = []
        self.ext_out = []

    def name(self, base):
        self.uid += 1
        return "%s_%d" % (base, self.uid)

    def din(self, name, shape, dt=F32):
        self.ext_in.append(name)
        return self.nc.dram_tensor(name, list(shape), dt, kind="ExternalInput").ap()

    def dout(self, name, shape, dt=F32):
        self.ext_out.append(name)
        return self.nc.dram_tensor(name, list(shape), dt, kind="ExternalOutput").ap()

    def dint(self, name, shape, dt=F32):
        return self.nc.dram_tensor(name, list(shape), dt, kind="Internal").ap()

    def sb(self, st, base, shape, dt):
        n = self.name(base)
        return Tn(st.enter_context(self.nc.sbuf_tensor(n, list(shape), dt)), n)

    def ps(self, st, base, shape, dt):
        n = self.name(base)
        return Tn(st.enter_context(self.nc.psum_tensor(n, list(shape), dt)), n)

    def mm(self, out, lhsT, rhs, start, stop, r, w):
        return self.P.op("pe", lambda e: e.matmul(out, lhsT=lhsT, rhs=rhs, start=start, stop=stop), _bufs(r), _bufs(w))

    def tr(self, out, in_, ident, r, w):
        return self.P.op("pe", lambda e: e.transpose(out, in_, ident), _bufs(r), _bufs(w))

    def act(self, out, in_, func, r, w, **kw):
        return self.P.op("act", lambda e: e.activation(out=out, in_=in_, func=func, **kw), _bufs(r), _bufs(w))

    def tt(self, eng, out, in0, in1, op, r, w):
        return self.P.op(eng, lambda e: e.tensor_tensor(out=out, in0=in0, in1=in1, op=op), _bufs(r), _bufs(w))

    def ts(self, eng, out, in0, s1, s2, op0, op1, r, w):
        if op1 is None:
            return self.P.op(eng, lambda e: e.tensor_scalar(out=out, in0=in0, scalar1=s1, scalar2=None, op0=op0), _bufs(r), _bufs(w))
        return self.P.op(eng, lambda e: e.tensor_scalar(out=out, in0=in0, scalar1=s1, scalar2=s2, op0=op0, op1=op1), _bufs(r), _bufs(w))

    def stt(self, eng, out, in0, scalar, in1, op0, op1, r, w):
        return self.P.op(eng, lambda e: e.scalar_tensor_tensor(out=out, in0=in0, scalar=scalar, in1=in1, op0=op0, op1=op1), _bufs(r), _bufs(w))

    def cp(self, eng, out, in_, r, w):
        if eng == "act":
            return self.act(out, in_, AF.Copy, r, w)
        return self.P.op(eng, lambda e: e.tensor_copy(out=out, in_=in_), _bufs(r), _bufs(w))

    def memset(self, eng, ap, val, w):
        return self.P.op(eng, lambda e: e.memset(ap, val), [], _bufs(w))

    def recip(self, out, in_, r, w):
        return self.P.op("dve", lambda e: e.reciprocal(out=out, in_=in_), _bufs(r), _bufs(w))

    def dma(self, q, out, in_, r=(), w=(), final=False):
        o = self.P.op(q, lambda e: e.dma_start(out=out, in_=in_), _bufs(r), _bufs(w))
        if final:
            self.outs.append(o)
        return o

    def finish(self):
        self.P.barrier()
        self.P.emit(final_wait_ops=self.outs)
        return self.nc


def chunk_list():
    return [(j * 512, 512) for j in range(NCH)] + [(NCH * 512, TAIL)]


def make_consts(cx, st):
    c = {}
    ident = cx.sb(st, "ident", [128, 128], BF16)
    cx.memset("pool", ident[:], 1.0, [ident])
    cx.P.op("pool", lambda e: e.affine_select(out=ident[:], in_=ident[:], pattern=[[-1, 128]],
                                              compare_op=ALU.is_equal, fill=0.0, base=0, channel_multiplier=1),
            [ident.b], [ident.b])
    c["ident"] = ident
    return c


def load_bcast(cx, st, name, vec_ap, n):
    t = cx.sb(st, name, [128, n], F32)
    cx.dma("sp", t[:], vec_ap.rearrange("(o n) -> o n", o=1).to_broadcast([128, n]), w=[t])
    return t


def rms_rows(cx, sc, src_ap, src_bufs, nrows, ncols, gb, out_ap, out_bufs, eng_sq="act"):
    junk, ss, rstd = sc["junk"], sc["ss"], sc["rstd"]
    cx.memset("dve", ss[0:nrows, :], 0.0, [ss])
    cx.act(junk[0:nrows, 0:ncols], src_ap, AF.Square, list(src_bufs) + [ss], [junk, ss], accum_out=ss[0:nrows, :])
    cx.ts("dve", rstd[0:nrows, :], ss[0:nrows, :], 1.0 / ncols, EPS, ALU.mult, ALU.add, [ss], [rstd])
    cx.act(rstd[0:nrows, :], rstd[0:nrows, :], AF.Sqrt, [rstd], [rstd])
    cx.recip(rstd[0:nrows, :], rstd[0:nrows, :], [rstd], [rstd])
    cx.stt("dve", out_ap, src_ap, rstd[0:nrows, 0:1], gb[0:nrows, 0:ncols], ALU.mult, ALU.mult,
           list(src_bufs) + [rstd, gb], out_bufs)


def transpose_rows(cx, consts, a_tok, nrows, nk, tp, aT, col0, evac="act"):
    ident = consts["ident"]
    for k in range(nk):
        cx.tr(tp[:, k, 0:nrows], a_tok[0:nrows, k * 128:(k + 1) * 128], ident[0:nrows, 0:nrows],
              [a_tok, ident], [tp])
    cx.cp(evac, aT[:, 0:nk, col0:col0 + nrows], tp[:, 0:nk, 0:nrows], [tp], [aT])


def load_w_cast(cx, dst, src_ap, nk, r=(), kstep=1):
    v = src_ap.rearrange("(k p) f -> p k f", p=128)
    for k in range(0, nk, kstep):
        k1 = min(nk, k + kstep)
        cx.dma("pq", dst[:, k:k1, :], v[:, k:k1, :], w=[dst])


def ffn_pass(cx, h_in, h_out, g_vec, wg, wu, wd, final_g=None, y_out=None):
    P = cx.P
    P.barrier()
    with ExitStack() as st:
        consts = make_consts(cx, st)
        Wg = cx.sb(st, "Wg", [128, 8, DFF], BF16)
        Wu = cx.sb(st, "Wu", [128, 8, DFF], BF16)
        Wd = cx.sb(st, "Wd", [128, NM, D], BF16)
        gb = load_bcast(cx, st, "gffn", g_vec, D)
        gf = load_bcast(cx, st, "gfin", final_g, D) if final_g is not None else None
        vg = wg.rearrange("(k p) f -> p k f", p=128)
        vu = wu.rearrange("(k p) f -> p k f", p=128)
        for k in range(8):
            cx.dma("pq", Wg[:, k, :], vg[:, k, :], w=[Wg])
            cx.dma("pq", Wu[:, k, :], vu[:, k, :], w=[Wu])
        load_w_cast(cx, Wd, wd, NM, kstep=2)
        NRING = 6
        hring = [cx.sb(st, "hblk", [128, D], F32) for _ in range(NRING)]
        a_tok = cx.sb(st, "atok", [128, D], BF16)
        aT = cx.sb(st, "aT", [128, 8, 512], BF16)
        gu = cx.sb(st, "gu", [128, NM, 512], BF16)
        sc = {"junk": cx.sb(st, "junk", [128, D], BF16), "ss": cx.sb(st, "ss", [128, 1], F32),
              "rstd": cx.sb(st, "rstd", [128, 1], F32)}
        sg = cx.sb(st, "sg", [128, 512], F32)
        tp = cx.ps(st, "tp", [128, 8, 128], BF16)
        pg = [cx.ps(st, "pg", [128, 512], F32) for _ in range(2)]
        pu = [cx.ps(st, "pu", [128, 512], F32) for _ in range(2)]
        pd = [cx.ps(st, "pd", [128, 512], F32) for _ in range(2)]
        ring_i = 0
        for (t0, T) in chunk_list():
            nb = (T + 127) // 128
            blks = []
            for tb in range(nb):
                bt = min(128, T - tb * 128)
                hb = hring[ring_i % NRING]
                ring_i += 1
                cx.dma("sp", hb[0:bt, :], h_in[t0 + tb * 128: t0 + tb * 128 + bt, :], w=[hb])
                blks.append((hb, bt))
            for tb, (hb, bt) in enumerate(blks):
                rms_rows(cx, sc, hb[0:bt, :], [hb], bt, D, gb, a_tok[0:bt, :], [a_tok])
                transpose_rows(cx, consts, a_tok, bt, 8, tp, aT, tb * 128)
            for m in range(NM):
                g_ps, u_ps = pg[m % 2], pu[m % 2]
                for k in range(8):
                    cx.mm(g_ps[:, 0:T], Wg[:, k, m * 128:(m + 1) * 128], aT[:, k, 0:T], k == 0, k == 7, [Wg, aT], [g_ps])
                for k in range(8):
                    cx.mm(u_ps[:, 0:T], Wu[:, k, m * 128:(m + 1) * 128], aT[:, k, 0:T], k == 0, k == 7, [Wu, aT], [u_ps])
                cx.act(sg[:, 0:T], g_ps[:, 0:T], AF.Silu, [g_ps], [sg])
                cx.tt("dve", gu[:, m, 0:T], sg[:, 0:T], u_ps[:, 0:T], ALU.mult, [sg, u_ps], [gu])
            for tb, (hb, bt) in enumerate(blks):
                for half in range(2):
                    d_ps = pd[half]
                    for m in range(NM):
                        cx.mm(d_ps[0:bt, :], gu[:, m, tb * 128: tb * 128 + bt], Wd[:, m, half * 512:(half + 1) * 512],
                              m == 0, m == NM - 1, [gu, Wd], [d_ps])
                    cx.tt("dve", hb[0:bt, half * 512:(half + 1) * 512], d_ps[0:bt, :], hb[0:bt, half * 512:(half + 1) * 512],
                          ALU.add, [d_ps, hb], [hb])
                r0 = t0 + tb * 128
                if h_out is not None:
                    cx.dma("pq", h_out[r0:r0 + bt, :], hb[0:bt, :], r=[hb], final=(y_out is None))
                if y_out is not None:
                    rms_rows_f32(cx, sc, hb, bt, gf)
                    cx.dma("pq", y_out[r0:r0 + bt, :], hb[0:bt, :], r=[hb], final=True)


def rms_rows_f32(cx, sc, hb, bt, gf):
    junk, ss, rstd = sc["junk"], sc["ss"], sc["rstd"]
    cx.memset("dve", ss[0:bt, :], 0.0, [ss])
    cx.act(junk[0:bt, :], hb[0:bt, :], AF.Square, [hb, ss], [junk, ss], accum_out=ss[0:bt, :])
    cx.ts("dve", rstd[0:bt, :], ss[0:bt, :], 1.0 / D, EPS, ALU.mult, ALU.add, [ss], [rstd])
    cx.act(rstd[0:bt, :], rstd[0:bt, :], AF.Sqrt, [rstd], [rstd])
    cx.recip(rstd[0:bt, :], rstd[0:bt, :], [rstd], [rstd])
    cx.stt("dve", hb[0:bt, :], hb[0:bt, :], rstd[0:bt, 0:1], gf[0:bt, :], ALU.mult, ALU.mult, [hb, rstd, gf], [hb])


POOL_W = (2, 4, 8, 16)


def pool_pass(cx, h_in, halo_in, h_out, g_vec, pool_w, pool_scale, bands, bhalo):
    P = cx.P
    P.barrier()
    with ExitStack() as st:
        gb = load_bcast(cx, st, "gmix", g_vec, D)
        psb = load_bcast(cx, st, "pscale", pool_scale, D)
        Bd = cx.sb(st, "bands", [128, 12, 128], BF16)
        cx.dma("pq", Bd[:], bands.rearrange("g t p f -> p (g t) f"), w=[Bd])
        Bh = cx.sb(st, "bhalo", [16, 4, 128], BF16)
        cx.dma("pq", Bh[:], bhalo.rearrange("g p f -> p g f"), w=[Bh])
        PW = cx.sb(st, "PW", [128, 8, 256], BF16)
        cx.dma("pq", PW[:], pool_w.rearrange("g (cc p) o -> p (g cc) o", p=128), w=[PW])
        hc = [cx.sb(st, "hc", [128, 4, D], F32) for _ in range(2)]
        hh = [cx.sb(st, "hh", [16, D], F32) for _ in range(2)]
        a_c = cx.sb(st, "a_c", [128, 4, D], BF16)
        a_h = cx.sb(st, "a_h", [16, D], BF16)
        pT = cx.sb(st, "pT", [128, 8, 512], BF16)
        tmp = cx.sb(st, "tmp", [128, 512], F32)
        sc = {"junk": cx.sb(st, "junk", [128, D], BF16), "ss": cx.sb(st, "ss", [128, 1], F32),
              "rstd": cx.sb(st, "rstd", [128, 1], F32)}
        pp = [cx.ps(st, "pp", [128, 512], F32) for _ in range(2)]
        mx = [cx.ps(st, "mx", [128, 512], F32) for _ in range(4)]
        for ci, (t0, T) in enumerate(chunk_list()):
            nb = (T + 127) // 128
            hcc, hhc = hc[ci % 2], hh[ci % 2]
            if T == 512:
                cx.dma("sp", hcc[:], h_in[t0:t0 + 512, :].rearrange("(b p) d -> p b d", p=128), w=[hcc])
            else:
                cx.dma("sp", hcc[0:T, 0, :], h_in[t0:t0 + T, :], w=[hcc])
            cx.dma("sp", hhc[:], halo_in[ci], w=[hhc])
            for tb in range(nb):
                bt = min(128, T - tb * 128)
                rms_rows(cx, sc, hcc[0:bt, tb, :], [hcc], bt, D, gb, a_c[0:bt, tb, :], [a_c])
            rms_rows(cx, sc, hhc[:, :], [hhc], 16, D, gb, a_h[:, :], [a_h])
            for g in range(4):
                for cc in range(2):
                    c0 = g * 256 + cc * 128
                    p_ps = pp[(g * 2 + cc) % 2]
                    for tb in range(nb):
                        bt = min(128, T - tb * 128)
                        bsel = 2 if (ci == 0 and tb == 0) else 0
                        cx.mm(p_ps[:, tb * 128: tb * 128 + bt], a_c[0:bt, tb, c0:c0 + 128], Bd[0:bt, g * 3 + bsel, 0:bt],
                              True, False, [a_c, Bd], [p_ps])
                        if tb == 0:
                            cx.mm(p_ps[:, 0:bt], a_h[0:16, c0:c0 + 128], Bh[0:16, g, 0:bt], False, True, [a_h, Bh], [p_ps])
                        else:
                            cx.mm(p_ps[:, tb * 128: tb * 128 + bt], a_c[:, tb - 1, c0:c0 + 128], Bd[:, g * 3 + 1, 0:bt],
                                  False, True, [a_c, Bd], [p_ps])
                    cx.cp("act", pT[:, g * 2 + cc, 0:T], p_ps[:, 0:T], [p_ps], [pT])
            for tb in range(nb):
                bt = min(128, T - tb * 128)
                for g in range(4):
                    m_ps = mx[(tb % 2) * 2 + g // 2]
                    col = (g % 2) * 256
                    for cc in range(2):
                        cx.mm(m_ps[0:bt, col:col + 256], pT[:, g * 2 + cc, tb * 128: tb * 128 + bt], PW[:, g * 2 + cc, :],
                              cc == 0, cc == 1, [pT, PW], [m_ps])
                for half in range(2):
                    m_ps = mx[(tb % 2) * 2 + half]
                    cx.tt("dve", tmp[0:bt, :], m_ps[0:bt, :], psb[0:bt, half * 512:(half + 1) * 512], ALU.mult, [m_ps, psb], [tmp])
                    cx.tt("dve", hcc[0:bt, tb, half * 512:(half + 1) * 512], tmp[0:bt, :], hcc[0:bt, tb, half * 512:(half + 1) * 512],
                          ALU.add, [tmp, hcc], [hcc])
            if T == 512:
                cx.dma("sp", h_out[t0:t0 + 512, :].rearrange("(b p) d -> p b d", p=128), hcc[:], r=[hcc])
            else:
                cx.dma("sp", h_out[t0:t0 + T, :], hcc[0:T, 0, :], r=[hcc])


def owned_chunks(r):
    return [2 * j + ((j % 2) if r == 0 else 1 - (j % 2)) for j in range(NCH)]


def owned_positions(r):
    pos = []
    for g in owned_chunks(r):
        pos.extend(range(g * 512, (g + 1) * 512))
    pos.extend(range(SEQ + TAIL * r, SEQ + TAIL * (r + 1)))
    return np.array(pos, dtype=np.int64)


def band_tables():
    bands = np.zeros((4, 3, 128, 128), np.float32)
    bhalo = np.zeros((4, 16, 128), np.float32)
    u = np.arange(128)[:, None]
    t = np.arange(128)[None, :]
    for g, w in enumerate(POOL_W):
        cur = ((u <= t) & (u > t - w)).astype(np.float32)
        bands[g, 0] = cur / w - np.eye(128, dtype=np.float32)
        bands[g, 1] = ((u - 128) > (t - w)).astype(np.float32) / w
        cnt = np.minimum(t + 1, w).astype(np.float32)
        bands[g, 2] = cur / cnt - np.eye(128, dtype=np.float32)
        i = np.arange(16)[:, None]
        bhalo[g] = ((i - 16) > (t - w)).astype(np.float32) / w
    return bands, bhalo


def chunk_norm_T(cx, consts, sc, hring, ring_state, h_in, t0, T, gb, a_tok, tp, aT):
    nb = (T + 127) // 128
    for tb in range(nb):
        bt = min(128, T - tb * 128)
        hb = hring[ring_state[0] % len(hring)]
        ring_state[0] += 1
        cx.dma("sp", hb[0:bt, :], h_in[t0 + tb * 128: t0 + tb * 128 + bt, :], w=[hb])
        rms_rows(cx, sc, hb[0:bt, :], [hb], bt, D, gb, a_tok[0:bt, :], [a_tok])
        transpose_rows(cx, consts, a_tok, bt, 8, tp, aT, tb * 128)


def proj_pass_qkv(cx, h_in, g_vec, w_qkv, QT, KT, V, fox=False, b_f=None, lfT=None):
    cx.P.barrier()
    NF = 3088 if fox else 3072
    with ExitStack() as st:
        consts = make_consts(cx, st)
        W = cx.sb(st, "Wqkv", [128, 8, NF], BF16)
        gb = load_bcast(cx, st, "gmix", g_vec, D)
        load_w_cast(cx, W, w_qkv, 8)
        hring = [cx.sb(st, "hblk", [128, D], F32) for _ in range(4)]
        a_tok = cx.sb(st, "atok", [128, D], BF16)
        aT = [cx.sb(st, "aT", [128, 8, 512], BF16) for _ in range(2)]
        sc = {"junk": cx.sb(st, "junk", [128, D], BF16), "ss": cx.sb(st, "ss", [128, 1], F32),
              "rstd": cx.sb(st, "rstd", [128, 1], F32)}
        qst = [cx.sb(st, "qst", [128, 8, 512], BF16) for _ in range(2)]
        kst = [cx.sb(st, "kst", [128, 8, 512], BF16) for _ in range(2)]
        vst = [cx.sb(st, "vst", [128, 4, D], BF16) for _ in range(2)]
        tp = cx.ps(st, "tp", [128, 8, 128], BF16)
        pq = [cx.ps(st, "pq", [128, 512], F32) for _ in range(4)]
        if fox:
            negb = cx.sb(st, "negb", [16, 1], F32)
            cx.dma("sp", negb[:], b_f.rearrange("(p o) -> p o", o=1), w=[negb])
            cx.ts("dve", negb[:], negb[:], -1.0, 0.0, ALU.mult, ALU.add, [negb], [negb])
            one16 = cx.sb(st, "one16", [16, 1], F32)
            cx.memset("pool", one16[:], 1.0, [one16])
            lfs = [cx.sb(st, "lfs", [16, 512], F32) for _ in range(2)]
            pf = cx.ps(st, "pf", [16, 512], F32)
        rs = [0]
        QTv = QT.rearrange("(oc p) t -> p oc t", p=128)
        KTv = KT.rearrange("(oc p) t -> p oc t", p=128)
        for ci, (t0, T) in enumerate(chunk_list()):
            nb = (T + 127) // 128
            aTc = aT[ci % 2]
            chunk_norm_T(cx, consts, sc, hring, rs, h_in, t0, T, gb, a_tok, tp, aTc)
            q_s, k_s, v_s = qst[ci % 2], kst[ci % 2], vst[ci % 2]
            for oc in range(16):
                ps_ = pq[oc % 4]
                for k in range(8):
                    cx.mm(ps_[:, 0:T], W[:, k, oc * 128:(oc + 1) * 128], aTc[:, k, 0:T], k == 0, k == 7, [W, aTc], [ps_])
                if oc < 8:
                    cx.act(q_s[:, oc, 0:T], ps_[:, 0:T], AF.Copy, [ps_], [q_s], scale=0.125)
                else:
                    cx.cp("dve", k_s[:, oc - 8, 0:T], ps_[:, 0:T], [ps_], [k_s])
            cx.dma("pq", QTv[:, :, t0:t0 + T], q_s[:, :, 0:T], r=[q_s], final=True)
            cx.dma("pq", KTv[:, :, t0:t0 + T], k_s[:, :, 0:T], r=[k_s], final=True)
            for tb in range(nb):
                bt = min(128, T - tb * 128)
                for half in range(2):
                    ps_ = pq[(tb * 2 + half) % 4]
                    for k in range(8):
                        cx.mm(ps_[0:bt, :], aTc[:, k, tb * 128: tb * 128 + bt], W[:, k, 2048 + half * 512: 2048 + (half + 1) * 512],
                              k == 0, k == 7, [aTc, W], [ps_])
                    if half == 0:
                        cx.cp("act", v_s[0:bt, tb, 0:512], ps_[0:bt, :], [ps_], [v_s])
                    else:
                        cx.cp("dve", v_s[0:bt, tb, 512:1024], ps_[0:bt, :], [ps_], [v_s])
            if T == 512:
                cx.dma("pq", V[t0:t0 + 512, :].rearrange("(b p) d -> p b d", p=128), v_s[:], r=[v_s], final=True)
            else:
                cx.dma("pq", V[t0:t0 + T, :], v_s[0:T, 0, :], r=[v_s], final=True)
            if fox:
                for k in range(8):
                    cx.mm(pf[:, 0:T], W[:, k, 3072:3088], aTc[:, k, 0:T], k == 0, k == 7, [W, aTc], [pf])
                l_s = lfs[ci % 2]
                cx.act(l_s[:, 0:T], pf[:, 0:T], AF.Exp, [pf, negb], [l_s], scale=-1.0, bias=negb[:, 0:1])
                cx.act(l_s[:, 0:T], l_s[:, 0:T], AF.Ln, [l_s, one16], [l_s], bias=one16[:, 0:1])
                cx.ts("dve", l_s[:, 0:T], l_s[:, 0:T], -1.0, 0.0, ALU.mult, ALU.add, [l_s], [l_s])
                cx.dma("pq", lfT[:, t0:t0 + T], l_s[:, 0:T], r=[l_s], final=True)


def owner(g):
    return 0 if (g % 2) == ((g // 2) % 2) else 1


def attn_pass(cx, kind, Qloc, Kg, Vg, mask, tmask, OT, scale=1.0, Krg=None, lfg=None, sel=None, scr=None, Vgt=None):
    P = cx.P
    P.barrier()
    R = {"sb": 64, "mla": 96, "fox": 70}[kind]
    RQ = 96 if kind == "mla" else 64
    KOFF = 32 if kind == "mla" else 0
    if kind == "fox":
        Fq_d, NFk_d = fox_prep(cx, lfg, sel, scr)
    with ExitStack() as st:
        Kt = [cx.sb(st, "Kt", [R, L], BF16) for _ in range(2)]
        Vt = [cx.sb(st, "Vt", [128, 65, 65], BF16) for _ in range(2)]
        Qt = [cx.sb(st, "Qt", [R, NT], BF16) for _ in range(2)]
        ost = [cx.sb(st, "ost", [64, NT], BF16) for _ in range(2)]
        mk = cx.sb(st, "mk", [128, 16, 512], BF16)
        cx.dma("pq", mk[:], mask.rearrange("a k p q -> p (a k) q"), w=[mk])
        tmk = cx.sb(st, "tmk", [16, 8], BF16)
        cx.dma("pq", tmk[:], tmask, w=[tmk])
        for v in Vt:
            cx.memset("pool", v[:, :, 64:65], 1.0, [v])
        zs = [cx.sb(st, "zs", [128, 512], F32) for _ in range(2)]
        pt = [cx.sb(st, "pt", [128, 512], BF16) for _ in range(3)]
        zps = [cx.ps(st, "zps", [128, 512], F32) for _ in range(2)]
        ops_ = [cx.ps(st, "ops", [128, 512], F32) for _ in range(2)]
        if kind == "sb":
            et = [cx.sb(st, "et", [128, 512], F32) for _ in range(2)]
            spt = [cx.sb(st, "spt", [128, 512], F32) for _ in range(2)]
            t1 = [cx.sb(st, "t1", [128, 512], F32) for _ in range(2)]
            acc = cx.sb(st, "acc", [128, 512], F32)
            U = cx.sb(st, "U", [128, 128], F32)
            ones = cx.sb(st, "ones", [128, 128], F32)
            one1 = cx.sb(st, "one1", [128, 1], F32)
            cx.memset("pool", one1[:], 1.0, [one1])
            cx.memset("pool", ones[:], 1.0, [ones])
            cx.memset("pool", U[:], 1.0, [U])
            cx.P.op("pool", lambda e: e.affine_select(out=U[:], in_=U[:], pattern=[[-1, 128]], compare_op=ALU.is_gt,
                                                      fill=0.0, base=0, channel_multiplier=1), [U.b], [U.b])
            lps = [cx.ps(st, "lps", [128, 512], F32) for _ in range(2)]
        else:
            osb = cx.sb(st, "osb", [65, 512], F32)
            rrow = cx.sb(st, "rrow", [65, 512], F32)
            onesr = cx.sb(st, "onesr", [65, 64], F32)
            cx.memset("pool", onesr[:], 1.0, [onesr])
            bps = cx.ps(st, "bps", [64, 512], F32)
        if kind == "mla":
            for kt in Kt:
                for g in range(16):
                    cx.dma("sp", kt[0:32, g * 512:(g + 1) * 512], Krg[owner(g), :, (g // 2) * 512:(g // 2 + 1) * 512], w=[kt])
                for r in range(2):
                    cx.dma("sp", kt[0:32, SEQ + TAIL * r: SEQ + TAIL * (r + 1)], Krg[r, :, NCH * 512: NT], w=[kt])
        if kind == "fox":
            for kt in Kt:
                cx.memset("pool", kt[64:70, :], 1.0, [kt])
            for qt in Qt:
                cx.memset("pool", qt[64:70, :], 1.0, [qt])
        def load_head(h):
            kt, vt, qt = Kt[h % 2], Vt[h % 2], Qt[h % 2]
            hr = (h % 2) * 64
            for g in range(16):
                j = g // 2
                cx.dma("sp", kt[KOFF:KOFF + 64, g * 512:(g + 1) * 512], Kg[h // 2, owner(g), hr:hr + 64, j * 512:(j + 1) * 512], w=[kt])
                cx.dma("sp", vt[:, 4 * g:4 * g + 4, 0:64],
                       Vg[j // 2, owner(g), (j % 2) * 512:(j % 2) * 512 + 512, h * 64:(h + 1) * 64].rearrange("(b p) d -> p b d", p=128), w=[vt])
            for r in range(2):
                cx.dma("sp", kt[KOFF:KOFF + 64, SEQ + TAIL * r: SEQ + TAIL * (r + 1)], Kg[h // 2, r, hr:hr + 64, NCH * 512:NT], w=[kt])
                cx.dma("sp", vt[TAIL * r:TAIL * (r + 1), 64, 0:64], Vgt[r, :, h * 64:(h + 1) * 64], w=[vt])
            cx.dma("sp", qt[0:RQ, :], Qloc[h * RQ:(h + 1) * RQ, :], w=[qt])
            if kind == "fox":
                cx.dma("sp", kt[67:70, :], NFk_d[h], w=[kt])
                cx.dma("sp", qt[64:67, :], Fq_d[h], w=[qt])

        items = []
        cc_ = 0
        for h in range(16):
            for ci, (t0, T) in enumerate(chunk_list()):
                if T == 512:
                    nkb = 8 * ci + 8
                    blocks = [(kb, 128, (kb - 8 * ci) if kb >= 8 * ci else -1) for kb in range(nkb)]
                    par = ci % 2
                else:
                    blocks = [(kb, 128, -1) for kb in range(64)] + [(64, 16, 0)]
                    par = -1
                if kind == "sb":
                    blocks = blocks[::-1]
                for bi, (kb, kn, mi) in enumerate(blocks):
                    items.append(dict(h=h, ci=ci, t0=t0, T=T, kb=kb, kn=kn, mi=mi, par=par, first=(bi == 0),
                                      last=(bi == len(blocks) - 1), cc=cc_, n=len(items),
                                      head_start=(ci == 0 and bi == 0), head_end=(ci == NCH and bi == len(blocks) - 1)))
                cc_ += 1
        NZ = 3
        zs3 = zs + [cx.sb(st, "zs", [128, 512], F32)]
        OR = 64 if kind == "sb" else 65
        if kind == "sb":
            et.append(cx.sb(st, "et", [128, 512], F32))
            spt.append(cx.sb(st, "spt", [128, 512], F32))
        zps.append(cx.ps(st, "zps", [128, 512], F32))

        def src_of(it):
            n, kn, T = it["n"], it["kn"], it["T"]
            if it["mi"] >= 0:
                return zs3[n % 3][0:kn, 0:T], zs3[n % 3]
            return zps[n % NZ][0:kn, 0:T], zps[n % NZ]

        def stage_z(it):
            n, h, kn, T, t0, kb = it["n"], it["h"], it["kn"], it["T"], it["t0"], it["kb"]
            kt, qt = Kt[h % 2], Qt[h % 2]
            z_ps = zps[n % NZ]
            cx.mm(z_ps[0:kn, 0:T], kt[0:R, kb * 128: kb * 128 + kn], qt[0:R, t0:t0 + T], True, True, [kt, qt], [z_ps])
            if it["mi"] >= 0:
                z_s = zs3[n % 3]
                m_ap = mk[:, it["par"] * 8 + it["mi"], :] if it["par"] >= 0 else tmk[0:16, 0:8]
                cx.tt("dve", z_s[0:kn, 0:T], z_ps[0:kn, 0:T], m_ap, ALU.add, [z_ps, mk, tmk], [z_s])

        def finalize(it):
            h, t0, T = it["h"], it["t0"], it["T"]
            o_ps, o_s = ops_[it["cc"] % 2], ost[h % 2]
            if kind == "sb":
                cx.cp("act", o_s[:, t0:t0 + T], o_ps[0:64, 0:T], [o_ps], [o_s])
            else:
                cx.cp("act", osb[:, 0:T], o_ps[0:65, 0:T], [o_ps], [osb])
                cx.recip(rrow[64:65, 0:T], osb[64:65, 0:T], [osb], [rrow])
                cx.mm(bps[:, 0:T], onesr[64:65, :], rrow[64:65, 0:T], True, True, [onesr, rrow], [bps])
                cx.tt("dve", o_s[:, t0:t0 + T], osb[0:64, 0:T], bps[:, 0:T], ALU.mult, [osb, bps], [o_s])
            if it["head_end"]:
                cx.dma("pq", OT[h * 64:(h + 1) * 64, :], o_s[:], r=[o_s], final=True)

        def stage_pv(it):
            n, h, kn, T, kb = it["n"], it["h"], it["kn"], it["T"], it["kb"]
            if it["head_start"] and h + 1 < 16:
                load_head(h + 1)
            vt = Vt[h % 2]
            p_t = pt[n % 3]
            o_ps = ops_[it["cc"] % 2]
            cx.mm(o_ps[0:OR, 0:T], vt[0:kn, kb, 0:OR], p_t[0:kn, 0:T], it["first"], it["last"], [vt, p_t], [o_ps])
            if it["last"]:
                finalize(it)

        if kind != "sb":
            def stage_exp(it):
                n, kn, T = it["n"], it["kn"], it["T"]
                src, srcb = src_of(it)
                cx.act(pt[n % 3][0:kn, 0:T], src, AF.Exp, [srcb], [pt[n % 3]], scale=scale)

            load_head(0)
            N = len(items)
            stage_z(items[0])
            if N > 1:
                stage_z(items[1])
            for n in range(N):
                if n + 2 < N:
                    stage_z(items[n + 2])
                stage_exp(items[n])
                stage_pv(items[n])
        else:
            def stage_sp(it):
                n, kn, T = it["n"], it["kn"], it["T"]
                src, srcb = src_of(it)
                e_t, s_t = et[n % 3], spt[n % 3]
                cx.act(e_t[0:kn, 0:T], src, AF.Exp, [srcb], [e_t])
                cx.act(s_t[0:kn, 0:T], e_t[0:kn, 0:T], AF.Ln, [e_t, one1], [s_t], bias=one1[0:kn, 0:1])

            def stage_later(it):
                n, kn, T = it["n"], it["kn"], it["T"]
                src, srcb = src_of(it)
                s_t, t_t, l_ps = spt[n % 3], t1[n % 2], lps[n % 2]
                first = it["first"]
                cx.mm(l_ps[0:kn, 0:T], U[0:kn, 0:kn], s_t[0:kn, 0:T], True, first, [U, s_t], [l_ps])
                if not first:
                    cx.mm(l_ps[0:kn, 0:T], ones[:, 0:kn], acc[:, 0:T], False, True, [ones, acc], [l_ps])
                cx.tt("dve", t_t[0:kn, 0:T], src, s_t[0:kn, 0:T], ALU.subtract, [srcb, s_t], [t_t])
                cx.tt("dve", t_t[0:kn, 0:T], t_t[0:kn, 0:T], l_ps[0:kn, 0:T], ALU.subtract, [t_t, l_ps], [t_t])
                if not it["last"]:
                    if first:
                        if kn < 128:
                            cx.memset("pool", acc[:, 0:T], 0.0, [acc])
                        cx.cp("pool", acc[0:kn, 0:T], s_t[0:kn, 0:T], [s_t], [acc])
                    else:
                        cx.tt("pool", acc[0:kn, 0:T], acc[0:kn, 0:T], s_t[0:kn, 0:T], ALU.add, [acc, s_t], [acc])

            def stage_a(it):
                n, kn, T = it["n"], it["kn"], it["T"]
                cx.act(pt[n % 3][0:kn, 0:T], t1[n % 2][0:kn, 0:T], AF.Exp, [t1[n % 2]], [pt[n % 3]])

            load_head(0)
            N = len(items)
            for k in range(min(3, N)):
                stage_z(items[k])
            for k in range(min(2, N)):
                stage_sp(items[k])
            stage_later(items[0])
            for n in range(N):
                if n + 3 < N:
                    stage_z(items[n + 3])
                if n + 2 < N:
                    stage_sp(items[n + 2])
                if n + 1 < N:
                    stage_later(items[n + 1])
                stage_a(items[n])
                stage_pv(items[n])


def fox_prep(cx, lfg, sel, scr):
    Fq_d, NFk_d = scr["Fq"], scr["NFk"]
    PIECE = 2052
    with ExitStack() as st:
        lf = cx.sb(st, "lf", [16, L], F32)
        F = cx.sb(st, "F", [16, L], F32)
        Fo = cx.sb(st, "Fo", [16, NT], F32)
        onesp = cx.sb(st, "onesp", [16, PIECE], F32)
        zero = cx.sb(st, "zero", [16, 1], F32)
        selt = cx.sb(st, "selt", [16, 18], F32)
        cx.dma("sp", selt[:], sel, w=[selt])
        cx.memset("pool", onesp[:], 1.0, [onesp])
        cx.memset("pool", zero[:], 0.0, [zero])
        for g in range(16):
            cx.dma("sp", lf[:, g * 512:(g + 1) * 512], lfg[owner(g), :, (g // 2) * 512:(g // 2 + 1) * 512], w=[lf])
        for r in range(2):
            cx.dma("sp", lf[:, SEQ + TAIL * r: SEQ + TAIL * (r + 1)], lfg[r, :, NCH * 512:NT], w=[lf])
        for i in range(L // PIECE):
            c0 = i * PIECE
            init = zero[:, 0:1] if i == 0 else F[:, c0 - 1:c0]
            cx.P.op("dve", (lambda c0=c0, init=init: (lambda e: e.tensor_tensor_scan(
                out=F[:, c0:c0 + PIECE], data0=onesp[:], data1=lf[:, c0:c0 + PIECE], initial=init,
                op0=ALU.mult, op1=ALU.add)))(), [lf.b, onesp.b, zero.b, F.b], [F.b])
        for j in range(NCH + 1):
            if j < NCH:
                a0, a1, n, d0 = 2 * j * 512, (2 * j + 1) * 512, 512, j * 512
            else:
                a0, a1, n, d0 = SEQ, SEQ + TAIL, TAIL, NCH * 512
            cx.ts("dve", Fo[:, d0:d0 + n], F[:, a0:a0 + n], selt[:, 2 * j:2 * j + 1], None, ALU.mult, None, [F, selt], [Fo])
            cx.stt("dve", Fo[:, d0:d0 + n], F[:, a1:a1 + n], selt[:, 2 * j + 1:2 * j + 2], Fo[:, d0:d0 + n], ALU.mult, ALU.add,
                   [F, selt, Fo], [Fo])
        hb = [cx.sb(st, "hb", [16, PIECE], BF16) for _ in range(3)]
        h32 = cx.sb(st, "h32", [16, PIECE], F32)
        rr = cx.sb(st, "rr", [16, PIECE], F32)

        def split(src, c0, n, dst, negate):
            s = -1.0 if negate else 1.0
            cx.ts("dve", rr[:, 0:n], src[:, c0:c0 + n], s, 0.0, ALU.mult, ALU.add, [src], [rr])
            for i in range(3):
                cx.cp("dve", hb[i][:, 0:n], rr[:, 0:n], [rr], [hb[i]])
                cx.dma("sp", dst[:, i, c0:c0 + n], hb[i][:, 0:n], r=[hb[i]])
                if i < 2:
                    cx.cp("dve", h32[:, 0:n], hb[i][:, 0:n], [hb[i]], [h32])
                    cx.tt("dve", rr[:, 0:n], rr[:, 0:n], h32[:, 0:n], ALU.subtract, [rr, h32], [rr])

        for i in range(L // PIECE):
            split(F, i * PIECE, PIECE, NFk_d, True)
        for i in range(2):
            split(Fo, i * PIECE, PIECE, Fq_d, False)
    cx.P.barrier()
    return Fq_d, NFk_d


def oproj_pass(cx, h_in, OT, w_o, h_out):
    cx.P.barrier()
    with ExitStack() as st:
        Wo = cx.sb(st, "Wo", [128, 8, D], BF16)
        load_w_cast(cx, Wo, w_o, 8)
        oc = [cx.sb(st, "oc", [128, 8, 512], BF16) for _ in range(2)]
        hc = [cx.sb(st, "hc", [128, 4, D], F32) for _ in range(2)]
        ps_ = [cx.ps(st, "po", [128, 512], F32) for _ in range(4)]
        OTv = OT.rearrange("(k p) t -> p k t", p=128)
        for ci, (t0, T) in enumerate(chunk_list()):
            nb = (T + 127) // 128
            o_c, h_c = oc[ci % 2], hc[ci % 2]
            cx.dma("sp", o_c[:, :, 0:T], OTv[:, :, t0:t0 + T], w=[o_c])
            if T == 512:
                cx.dma("sp", h_c[:], h_in[t0:t0 + 512, :].rearrange("(b p) d -> p b d", p=128), w=[h_c])
            else:
                cx.dma("sp", h_c[0:T, 0, :], h_in[t0:t0 + T, :], w=[h_c])
            for tb in range(nb):
                bt = min(128, T - tb * 128)
                for half in range(2):
                    p_ = ps_[(tb * 2 + half) % 4]
                    for k in range(8):
                        cx.mm(p_[0:bt, :], o_c[:, k, tb * 128: tb * 128 + bt], Wo[:, k, half * 512:(half + 1) * 512],
                              k == 0, k == 7, [o_c, Wo], [p_])
                    cx.tt("dve", h_c[0:bt, tb, half * 512:(half + 1) * 512], p_[0:bt, :], h_c[0:bt, tb, half * 512:(half + 1) * 512],
                          ALU.add, [p_, h_c], [h_c])
            if T == 512:
                cx.dma("pq", h_out[t0:t0 + 512, :].rearrange("(b p) d -> p b d", p=128), h_c[:], r=[h_c])
            else:
                cx.dma("pq", h_out[t0:t0 + T, :], h_c[0:T, 0, :], r=[h_c])


def mask_tables(r, strict):
    m = np.zeros((2, 8, 128, 512), np.float32)
    oc = owned_chunks(r)
    for par in range(2):
        j = par
        qpos = oc[j] * 512 + np.arange(512)[None, :]
        for kbl in range(8):
            kpos = (2 * j) * 512 + kbl * 128 + np.arange(128)[:, None]
            ok = (kpos < qpos) if strict else (kpos <= qpos)
            m[par, kbl] = np.where(ok, 0.0, NEG)
    qpos = SEQ + TAIL * r + np.arange(TAIL)[None, :]
    kpos = SEQ + np.arange(16)[:, None]
    ok = (kpos < qpos) if strict else (kpos <= qpos)
    tm = np.where(ok, 0.0, NEG).astype(np.float32)
    return m, tm


def build_launch1():
    cx = Ctx()
    h_in = cx.din("h_in", [NT, D])
    halo = cx.din("halo", [NCH + 1, 16, D])
    bands = cx.din("bands", [4, 3, 128, 128])
    bhalo = cx.din("bhalo", [4, 16, 128])
    g_mix = cx.din("g_mix", [D])
    g_ffn = cx.din("g_ffn", [D])
    pool_w = cx.din("pool_w", [4, 256, 256])
    pool_scale = cx.din("pool_scale", [D])
    wg = cx.din("wg", [D, DFF])
    wu = cx.din("wu", [D, DFF])
    wd = cx.din("wd", [DFF, D])
    g_mix1 = cx.din("g_mix1", [D])
    w_qkv = cx.din("w_qkv", [D, 3072])
    h_mid = cx.dint("h_mid", [NT, D])
    h_out = cx.dout("h_out", [NT, D])
    QT = cx.dout("QT", [D, NT], BF16)
    KT = cx.dout("KT", [D, NT], BF16)
    V = cx.dout("V", [NT, D], BF16)
    pool_pass(cx, h_in, halo, h_mid, g_mix, pool_w, pool_scale, bands, bhalo)
    ffn_pass(cx, h_mid, h_out, g_ffn, wg, wu, wd)
    proj_pass_qkv(cx, h_out, g_mix1, w_qkv, QT, KT, V)
    return cx.finish()


def proj_pass_mla(cx, h_in, g_vec, w_down, q_norm, kv_norm, w_uq, w_ukv, cs_tm, csT, Qloc, Kn, Kr, V, level=9):
    cx.P.barrier()
    with ExitStack() as st:
        consts = make_consts(cx, st)
        Wdn = cx.sb(st, "Wdn", [128, 8, 672], BF16)
        load_w_cast(cx, Wdn, w_down, 8, kstep=4)
        WuqP = cx.sb(st, "WuqP", [128, 3, 16, 96], BF16)
        WuqS = cx.sb(st, "WuqS", [128, 3, 16, 32], BF16)
        WukN = cx.sb(st, "WukN", [128, 2, 16, 64], BF16)
        WuV = cx.sb(st, "WuV", [128, 2, 16, 64], BF16)
        with ExitStack() as st2:
            Wq_nat = cx.sb(st2, "Wq_nat", [128, 3, 16, 96], BF16)
            Wkv_nat = cx.sb(st2, "Wkv_nat", [128, 2, 16, 2, 64], BF16)
            uqv = w_uq.rearrange("(k p) (h c) -> p k h c", p=128, c=96)
            ukv = w_ukv.rearrange("(k p) (h t c) -> p k h t c", p=128, t=2, c=64)
            for k in range(3):
                cx.dma("pq", Wq_nat[:, k, :, :], uqv[:, k, :, :], w=[Wq_nat])
            for k in range(2):
                cx.dma("pq", Wkv_nat[:, k, :, :, :], ukv[:, k, :, :, :], w=[Wkv_nat])
            for k in range(3 if "W" not in os.environ.get("MLA_SKIP", "") else 0):
                cx.cp("dve", WuqP[:, k, :, 0:32], Wq_nat[:, k, :, 64:96], [Wq_nat], [WuqP])
                cx.cp("pool", WuqP[:, k, :, 32:96], Wq_nat[:, k, :, 0:64], [Wq_nat], [WuqP])
                cx.cp("dve", WuqS[:, k, :, 0:16], Wq_nat[:, k, :, 80:96], [Wq_nat], [WuqS])
                cx.cp("pool", WuqS[:, k, :, 16:32], Wq_nat[:, k, :, 64:80], [Wq_nat], [WuqS])
            for k in range(2 if "W" not in os.environ.get("MLA_SKIP", "") else 0):
                cx.cp("dve", WukN[:, k, :, :], Wkv_nat[:, k, :, 0, :], [Wkv_nat], [WukN])
                cx.cp("pool", WuV[:, k, :, :], Wkv_nat[:, k, :, 1, :], [Wkv_nat], [WuV])
            cx.P.barrier()
        gb = load_bcast(cx, st, "gmix", g_vec, D)
        qnb = load_bcast(cx, st, "qnb", q_norm, 384)
        kvnb = load_bcast(cx, st, "kvnb", kv_norm, 256)
        hring = [cx.sb(st, "hblk", [128, D], F32) for _ in range(4)]
        a_tok = cx.sb(st, "atok", [128, D], BF16)
        aT = [cx.sb(st, "aT", [128, 8, 512], BF16) for _ in range(2)]
        sc = {"junk": cx.sb(st, "junk", [128, D], BF16), "ss": cx.sb(st, "ss", [128, 1], F32),
              "rstd": cx.sb(st, "rstd", [128, 1], F32)}
        m_tok = cx.sb(st, "mtok", [128, 768], BF16)
        cqT = cx.sb(st, "cqT", [128, 3, 512], BF16)
        ckvT = cx.sb(st, "ckvT", [128, 2, 512], BF16)
        krT = [cx.sb(st, "krT", [32, 512], BF16) for _ in range(2)]
        qst = [cx.sb(st, "qst", [96, 16, 512], BF16) for _ in range(2)]
        kst = [cx.sb(st, "kst", [128, 8, 512], BF16) for _ in range(2)]
        vst = [cx.sb(st, "vst", [128, 4, D], BF16) for _ in range(2)]
        cst = [cx.sb(st, "cst", [128, 64], F32) for _ in range(2)]
        csf = [cx.sb(st, "csf", [64, 512], F32) for _ in range(2)]
        snf = [cx.sb(st, "snf", [32, 512], F32) for _ in range(2)]
        kr32 = cx.sb(st, "kr32", [128, 32], F32)
        kr32b = cx.sb(st, "kr32b", [128, 32], F32)
        tA = cx.sb(st, "tA", [32, 512], F32)
        tB = cx.sb(st, "tB", [32, 512], F32)
        tp = cx.ps(st, "tp", [128, 8, 128], BF16)
        dn = [cx.ps(st, "dn", [128, 512], F32) for _ in range(2)]
        pq = [cx.ps(st, "pq", [128, 512], F32) for _ in range(4)]
        rs = [0]
        Qv = Qloc.rearrange("(h c) t -> c h t", c=96)
        Knv = Kn.rearrange("(oc p) t -> p oc t", p=128)
        WukNf = lambda k, oc: WukN[:, k, 2 * oc:2 * oc + 2, :]
        pi = 0
        for ci, (t0, T) in enumerate(chunk_list()):
            nb = (T + 127) // 128
            aTc = aT[ci % 2]
            chunk_norm_T(cx, consts, sc, hring, rs, h_in, t0, T, gb, a_tok, tp, aTc)
            if "C" in os.environ.get("MLA_SKIP", ""):
                cx.dma("pq", Kr[:, t0:t0 + T], aTc[0:32, 0, 0:T], r=[aTc], final=True)
                continue
            cf, sf = csf[ci % 2], snf[ci % 2]
            if "D" not in os.environ.get("MLA_SKIP", ""):
                cx.dma("sp", cf[:, 0:T], csT[:, t0:t0 + T], w=[cf])
                cx.dma("sp", sf[:, 0:T], csT[32:64, t0:t0 + T], w=[sf])
            krTc = krT[ci % 2]
            for tb in range(nb):
                bt = min(128, T - tb * 128)
                ct = cst[tb % 2]
                if "D" not in os.environ.get("MLA_SKIP", ""):
                    cx.dma("sp", ct[0:bt, :], cs_tm[t0 + tb * 128: t0 + tb * 128 + bt, :], w=[ct])
                SKIP = os.environ.get("MLA_SKIP", "")
                for k in range(8 if "M" not in SKIP else 0):
                    cx.mm(dn[0][0:bt, 0:384], aTc[:, k, tb * 128: tb * 128 + bt], Wdn[:, k, 0:384], k == 0, k == 7, [aTc, Wdn], [dn[0]])
                for k in range(8 if "M" not in SKIP else 0):
                    cx.mm(dn[1][0:bt, 0:288], aTc[:, k, tb * 128: tb * 128 + bt], Wdn[:, k, 384:672], k == 0, k == 7, [aTc, Wdn], [dn[1]])
                if "M" in SKIP:
                    cx.cp("dve", m_tok[0:bt, 0:672], a_tok[0:bt, 0:672], [a_tok], [m_tok])
                if "M" in SKIP:
                    pass
                elif "n" in SKIP:
                    cx.cp("dve", m_tok[0:bt, 0:384], dn[0][0:bt, 0:384], [dn[0]], [m_tok])
                    cx.cp("dve", m_tok[0:bt, 384:640], dn[1][0:bt, 0:256], [dn[1]], [m_tok])
                else:
                    rms_rows(cx, sc, dn[0][0:bt, 0:384], [dn[0]], bt, 384, qnb, m_tok[0:bt, 0:384], [m_tok])
                    rms_rows(cx, sc, dn[1][0:bt, 0:256], [dn[1]], bt, 256, kvnb, m_tok[0:bt, 384:640], [m_tok])
                if "M" in SKIP:
                    pass
                elif "r" in SKIP:
                    cx.cp("dve", m_tok[0:bt, 640:672], dn[1][0:bt, 256:288], [dn[1]], [m_tok])
                else:
                    cx.tt("dve", kr32[0:bt, :], dn[1][0:bt, 256:288], ct[0:bt, 0:32], ALU.mult, [dn[1], ct], [kr32])
                    cx.tt("dve", kr32b[0:bt, 0:16], dn[1][0:bt, 272:288], ct[0:bt, 32:48], ALU.mult, [dn[1], ct], [kr32b])
                    cx.tt("dve", kr32b[0:bt, 16:32], dn[1][0:bt, 256:272], ct[0:bt, 48:64], ALU.mult, [dn[1], ct], [kr32b])
                    cx.tt("dve", m_tok[0:bt, 640:672], kr32[0:bt, :], kr32b[0:bt, :], ALU.add, [kr32, kr32b], [m_tok])
                ident = consts["ident"]
                if "T" in SKIP:
                    cx.cp("act", krTc[:, tb * 128: tb * 128 + bt], aTc[0:32, 0, tb * 128: tb * 128 + bt], [aTc, m_tok], [krTc])
                    continue
                for k in range(5):
                    cx.tr(tp[:, k, 0:bt], m_tok[0:bt, k * 128:(k + 1) * 128], ident[0:bt, 0:bt], [m_tok, ident], [tp])
                if "t" in SKIP:
                    cx.tr(tp[:, 5, 0:bt], m_tok[0:bt, 640:768], ident[0:bt, 0:bt], [m_tok, ident], [tp])
                else:
                    cx.tr(tp[0:32, 5, 0:bt], m_tok[0:bt, 640:672], ident[0:bt, 0:bt], [m_tok, ident], [tp])
                cx.cp("act", cqT[:, :, tb * 128: tb * 128 + bt], tp[:, 0:3, 0:bt], [tp], [cqT])
                cx.cp("act", ckvT[:, :, tb * 128: tb * 128 + bt], tp[:, 3:5, 0:bt], [tp], [ckvT])
                cx.cp("act", krTc[:, tb * 128: tb * 128 + bt], tp[0:32, 5, 0:bt], [tp], [krTc])
            cx.dma("pq", Kr[:, t0:t0 + T], krTc[:, 0:T], r=[krTc], final=True)
            q_s, k_s, v_s = qst[ci % 2], kst[ci % 2], vst[ci % 2]
            if level < 2:
                continue
            for h in range(16):
                ph, psw = pq[pi % 4], pq[(pi + 1) % 4]
                pi += 2
                for k in range(3):
                    cx.mm(ph[0:96, 0:T], WuqP[:, k, h, :], cqT[:, k, 0:T], k == 0, k == 2, [WuqP, cqT], [ph])
                for k in range(3):
                    cx.mm(psw[0:32, 0:T], WuqS[:, k, h, :], cqT[:, k, 0:T], k == 0, k == 2, [WuqS, cqT], [psw])
                cx.tt("dve", tA[:, 0:T], ph[0:32, 0:T], cf[0:32, 0:T], ALU.mult, [ph, cf], [tA])
                cx.tt("dve", tB[:, 0:T], psw[0:32, 0:T], sf[:, 0:T], ALU.mult, [psw, sf], [tB])
                cx.tt("pool", q_s[0:32, h, 0:T], tA[:, 0:T], tB[:, 0:T], ALU.add, [tA, tB], [q_s])
                cx.cp("act", q_s[32:64, h, 0:T], ph[32:64, 0:T], [ph], [q_s])
                cx.cp("act", q_s[64:96, h, 0:T], ph[64:96, 0:T], [ph], [q_s])
            cx.dma("pq", Qv[:, :, t0:t0 + T], q_s[:, :, 0:T], r=[q_s], final=True)
            if level < 3:
                continue
            for oc in range(8):
                ps_ = pq[pi % 4]
                pi += 1
                for k in range(2):
                    cx.mm(ps_[:, 0:T], WukN[:, k, 2 * oc:2 * oc + 2, :], ckvT[:, k, 0:T], k == 0, k == 1, [WukN, ckvT], [ps_])
                if oc % 2 == 0:
                    cx.cp("act", k_s[:, oc, 0:T], ps_[:, 0:T], [ps_], [k_s])
                else:
                    cx.cp("dve", k_s[:, oc, 0:T], ps_[:, 0:T], [ps_], [k_s])
            cx.dma("pq", Knv[:, :, t0:t0 + T], k_s[:, :, 0:T], r=[k_s], final=True)
            if level < 4:
                continue
            for tb in range(nb):
                bt = min(128, T - tb * 128)
                for half in range(2):
                    ps_ = pq[pi % 4]
                    pi += 1
                    for k in range(2):
                        cx.mm(ps_[0:bt, :], ckvT[:, k, tb * 128: tb * 128 + bt], WuV[:, k, half * 8:(half + 1) * 8, :],
                              k == 0, k == 1, [ckvT, WuV], [ps_])
                    if half == 0:
                        cx.cp("act", v_s[0:bt, tb, 0:512], ps_[0:bt, :], [ps_], [v_s])
                    else:
                        cx.cp("dve", v_s[0:bt, tb, 512:1024], ps_[0:bt, :], [ps_], [v_s])
            if T == 512:
                cx.dma("pq", V[t0:t0 + 512, :].rearrange("(b p) d -> p b d", p=128), v_s[:], r=[v_s], final=True)
            else:
                cx.dma("pq", V[t0:t0 + T, :], v_s[0:T, 0, :], r=[v_s], final=True)


def rope_tables(r):
    pos = owned_positions(r).astype(np.float32)
    inv = (np.float32(10000.0) ** (-np.arange(0, 32, 2, dtype=np.float32) / np.float32(32))).astype(np.float32)
    ang = (pos[:, None] * inv[None, :]).astype(np.float32)
    c, s = np.cos(ang).astype(np.float32), np.sin(ang).astype(np.float32)
    tm = np.concatenate([c, c, -s, s], axis=1).astype(np.float32)
    return np.ascontiguousarray(tm), np.ascontiguousarray(tm.T)


def _ffn_inputs(cx):
    return (cx.din("g_ffn", [D]), cx.din("wg", [D, DFF]), cx.din("wu", [D, DFF]), cx.din("wd", [DFF, D]))


def _attn_inputs(cx, qrows):
    return dict(h_in=cx.din("h_in", [NT, D]), QT=cx.din("QT", [qrows, NT], BF16), Kg=cx.din("Kg", [2, D, NT], BF16),
                Vg=cx.din("Vg", [2, NT, D], BF16), mask=cx.din("mask", [2, 8, 128, 512]), tmask=cx.din("tmask", [16, 8]),
                w_o=cx.din("w_o", [D, D]))


def build_launch2():
    cx = Ctx()
    a = _attn_inputs(cx, D)
    g_ffn, wg, wu, wd = _ffn_inputs(cx)
    g_mix = cx.din("g_mix", [D])
    w_down = cx.din("w_down", [D, 672])
    q_norm = cx.din("q_norm", [384])
    kv_norm = cx.din("kv_norm", [256])
    w_uq = cx.din("w_uq", [384, 1536])
    w_ukv = cx.din("w_ukv", [256, 2048])
    cs_tm = cx.din("cs_tm", [NT, 64])
    csT = cx.din("csT", [64, NT])
    OT = cx.dint("OT", [D, NT], BF16)
    h_mid = cx.dint("h_mid", [NT, D])
    h_out = cx.dout("h_out", [NT, D])
    Qn = cx.dout("Qn", [16 * 96, NT], BF16)
    Kn = cx.dout("Kn", [D, NT], BF16)
    Kr = cx.dout("Kr", [32, NT], BF16)
    Vn = cx.dout("Vn", [NT, D], BF16)
    attn_pass(cx, "sb", a["QT"], a["Kg"], a["Vg"], a["mask"], a["tmask"], OT)
    oproj_pass(cx, a["h_in"], OT, a["w_o"], h_mid)
    ffn_pass(cx, h_mid, h_out, g_ffn, wg, wu, wd)
    proj_pass_mla(cx, h_out, g_mix, w_down, q_norm, kv_norm, w_uq, w_ukv, cs_tm, csT, Qn, Kn, Kr, Vn)
    return cx.finish()


def build_launch3():
    cx = Ctx()
    a = _attn_inputs(cx, 16 * 96)
    Krg = cx.din("Krg", [2, 32, NT], BF16)
    g_ffn, wg, wu, wd = _ffn_inputs(cx)
    g_mix = cx.din("g_mix", [D])
    w_qkvf = cx.din("w_qkvf", [D, 3088])
    b_f = cx.din("b_f", [16])
    OT = cx.dint("OT", [D, NT], BF16)
    h_mid = cx.dint("h_mid", [NT, D])
    h_out = cx.dout("h_out", [NT, D])
    Qn = cx.dout("Qn", [D, NT], BF16)
    Kn = cx.dout("Kn", [D, NT], BF16)
    Vn = cx.dout("Vn", [NT, D], BF16)
    lfT = cx.dout("lfT", [16, NT], F32)
    attn_pass(cx, "mla", a["QT"], a["Kg"], a["Vg"], a["mask"], a["tmask"], OT, scale=float(96 ** -0.5), Krg=Krg)
    oproj_pass(cx, a["h_in"], OT, a["w_o"], h_mid)
    ffn_pass(cx, h_mid, h_out, g_ffn, wg, wu, wd)
    proj_pass_qkv(cx, h_out, g_mix, w_qkvf, Qn, Kn, Vn, fox=True, b_f=b_f, lfT=lfT)
    return cx.finish()


def build_launch4():
    cx = Ctx()
    a = _attn_inputs(cx, D)
    lfg = cx.din("lfg", [2, 16, NT], F32)
    sel = cx.din("sel", [16, 18], F32)
    g_ffn, wg, wu, wd = _ffn_inputs(cx)
    g_fin = cx.din("g_fin", [D])
    OT = cx.dint("OT", [D, NT], BF16)
    h_mid = cx.dint("h_mid", [NT, D])
    scr = {"Fq": cx.dint("Fq", [16, 3, NT], BF16), "NFk": cx.dint("NFk", [16, 3, L], BF16)}
    y = cx.dout("y", [NT, D])
    attn_pass(cx, "fox", a["QT"], a["Kg"], a["Vg"], a["mask"], a["tmask"], OT, lfg=lfg, sel=sel, scr=scr)
    oproj_pass(cx, a["h_in"], OT, a["w_o"], h_mid)
    ffn_pass(cx, h_mid, None, g_ffn, wg, wu, wd, final_g=g_fin, y_out=y)
    return cx.finish()


_NC_CACHE = {}


def _get_nc(name, fn):
    if name not in _NC_CACHE:
        _NC_CACHE[name] = fn()
    return _NC_CACHE[name]


def _run(nc, in_maps, tag=""):
    res = run_bass_kernel_spmd(nc, in_maps, core_ids=list(range(8)))
    dbg = os.environ.get("MK_DBG")
    if dbg:
        os.makedirs(dbg, exist_ok=True)
        for k in res.results[0]:
            a = np.stack([np.asarray(res.results[c][k]) for c in range(2)])
            if a.dtype != np.float32:
                a = a.view(np.uint16)
            np.save(os.path.join(dbg, "%s_%s.npy" % (tag, k)), a)
    return res.results


def _pair(arrs, c):
    b = c // 2
    return np.stack([np.asarray(arrs[2 * b]), np.asarray(arrs[2 * b + 1])])


def kernel_unfused(x, meta, norm_mix, norm_ffn, pool_w, pool_scale, sb_w_qkv, sb_w_o,
           mla_w_down, mla_q_norm, mla_kv_norm, mla_w_uq, mla_w_ukv, mla_w_o,
           fox_w_qkvf, fox_b_f, fox_w_o, ffn_w_gate, ffn_w_up, ffn_w_down, final_norm):
    f = lambda a: np.ascontiguousarray(np.asarray(a, dtype=np.float32))
    x, meta = f(x), f(meta)
    norm_mix, norm_ffn = f(norm_mix), f(norm_ffn)
    wg, wu, wd = f(ffn_w_gate), f(ffn_w_up), f(ffn_w_down)
    bands, bhalo = band_tables()
    cores = list(range(8))
    in_maps = []
    for c in cores:
        b, r = c // 2, c % 2
        hfull = np.concatenate([meta, x[b]], axis=0)
        pos = owned_positions(r)
        starts = [g * 512 for g in owned_chunks(r)] + [SEQ + TAIL * r]
        halo = np.zeros((NCH + 1, 16, D), np.float32)
        for i, s0 in enumerate(starts):
            lo = max(0, s0 - 16)
            if s0 > 0:
                halo[i, 16 - (s0 - lo):] = hfull[lo:s0]
        bd = bands.copy()
        if r == 1:
            bd[:, 2] = bd[:, 0]
        in_maps.append({"h_in": np.ascontiguousarray(hfull[pos]), "halo": halo, "bands": bd, "bhalo": bhalo,
                        "g_mix": norm_mix[0], "g_ffn": norm_ffn[0], "pool_w": f(pool_w)[0], "pool_scale": f(pool_scale)[0],
                        "wg": wg[0], "wu": wu[0], "wd": wd[0], "g_mix1": norm_mix[1], "w_qkv": f(sb_w_qkv)[0]})
    r1 = _run(_get_nc("l1", build_launch1), in_maps, "l1")
    QT = [r1[c]["QT"] for c in cores]
    KT = [r1[c]["KT"] for c in cores]
    VV = [r1[c]["V"] for c in cores]
    in_maps = []
    for c in cores:
        r = c % 2
        m, tm = mask_tables(r, strict=True)
        cs_tm, csT = rope_tables(r)
        in_maps.append({"h_in": r1[c]["h_out"], "QT": QT[c], "Kg": _pair(KT, c), "Vg": _pair(VV, c), "mask": m, "tmask": tm,
                        "w_o": f(sb_w_o)[0], "g_ffn": norm_ffn[1], "wg": wg[1], "wu": wu[1], "wd": wd[1],
                        "g_mix": norm_mix[2], "w_down": f(mla_w_down)[0], "q_norm": f(mla_q_norm)[0], "kv_norm": f(mla_kv_norm)[0],
                        "w_uq": f(mla_w_uq)[0], "w_ukv": f(mla_w_ukv)[0], "cs_tm": cs_tm, "csT": csT})
    r2 = _run(_get_nc("l2", build_launch2), in_maps, "l2")
    del r1
    Qn = [r2[c]["Qn"] for c in cores]
    Kn = [r2[c]["Kn"] for c in cores]
    Kr = [r2[c]["Kr"] for c in cores]
    Vn = [r2[c]["Vn"] for c in cores]
    in_maps = []
    for c in cores:
        r = c % 2
        m, tm = mask_tables(r, strict=False)
        in_maps.append({"h_in": r2[c]["h_out"], "QT": Qn[c], "Kg": _pair(Kn, c), "Vg": _pair(Vn, c), "Krg": _pair(Kr, c),
                        "mask": m, "tmask": tm, "w_o": f(mla_w_o)[0], "g_ffn": norm_ffn[2], "wg": wg[2], "wu": wu[2], "wd": wd[2],
                        "g_mix": norm_mix[3], "w_qkvf": f(fox_w_qkvf)[0], "b_f": f(fox_b_f)[0]})
    r3 = _run(_get_nc("l3", build_launch3), in_maps, "l3")
    del r2
    Qn = [r3[c]["Qn"] for c in cores]
    Kn = [r3[c]["Kn"] for c in cores]
    Vn = [r3[c]["Vn"] for c in cores]
    lf = [r3[c]["lfT"] for c in cores]
    in_maps = []
    for c in cores:
        r = c % 2
        m, tm = mask_tables(r, strict=False)
        sel = np.zeros((16, 18), np.float32)
        oc = owned_chunks(r)
        for j in range(NCH):
            sel[:, 2 * j + (oc[j] - 2 * j)] = 1.0
        sel[:, 16 + r] = 1.0
        in_maps.append({"h_in": r3[c]["h_out"], "QT": Qn[c], "Kg": _pair(Kn, c), "Vg": _pair(Vn, c), "lfg": _pair(lf, c),
                        "sel": sel, "mask": m, "tmask": tm, "w_o": f(fox_w_o)[0], "g_ffn": norm_ffn[3],
                        "wg": wg[3], "wu": wu[3], "wd": wd[3], "g_fin": f(final_norm)})
    r4 = _run(_get_nc("l4", build_launch4), in_maps, "l4")
    del r3
    out = np.zeros((4, SEQ, D), np.float32)
    for c in cores:
        b, r = c // 2, c % 2
        pos = owned_positions(r)
        y = np.asarray(r4[c]["y"])
        keep = pos >= NMETA
        out[b, pos[keep] - NMETA] = y[keep]
    return out


PAIRS = [[0, 1], [2, 3], [4, 5], [6, 7]]


def all_gather(cx, pairs):
    cx.P.barrier()
    for (src, dst) in pairs:
        cx.P.op("cc", (lambda src=src, dst=dst: (lambda e: e.collective_compute(
            "AllGather", ALU.bypass, replica_groups=PAIRS, ins=[src.opt()], outs=[dst.opt()])))(), [], [])
    cx.P.barrier()


def build_fused():
    cx = Ctx()
    h_in = cx.din("h_in", [NT, D])
    halo = cx.din("halo", [NCH + 1, 16, D])
    bands = cx.din("bands", [4, 3, 128, 128])
    bhalo = cx.din("bhalo", [4, 16, 128])
    norm_mix = cx.din("norm_mix", [4, D])
    norm_ffn = cx.din("norm_ffn", [4, D])
    pool_w = cx.din("pool_w", [4, 256, 256])
    pool_scale = cx.din("pool_scale", [D])
    wg = cx.din("wg", [4, D, DFF])
    wu = cx.din("wu", [4, D, DFF])
    wd = cx.din("wd", [4, DFF, D])
    sb_w_qkv = cx.din("sb_w_qkv", [D, 3072])
    sb_w_o = cx.din("sb_w_o", [D, D])
    w_down = cx.din("mla_w_down", [D, 672])
    q_norm = cx.din("mla_q_norm", [384])
    kv_norm = cx.din("mla_kv_norm", [256])
    w_uq = cx.din("mla_w_uq", [384, 1536])
    w_ukv = cx.din("mla_w_ukv", [256, 2048])
    mla_w_o = cx.din("mla_w_o", [D, D])
    fox_w_qkvf = cx.din("fox_w_qkvf", [D, 3088])
    fox_b_f = cx.din("fox_b_f", [16])
    fox_w_o = cx.din("fox_w_o", [D, D])
    g_fin = cx.din("g_fin", [D])
    mask_lt = cx.din("mask_lt", [2, 8, 128, 512])
    tmask_lt = cx.din("tmask_lt", [16, 8])
    mask_le = cx.din("mask_le", [2, 8, 128, 512])
    tmask_le = cx.din("tmask_le", [16, 8])
    cs_tm = cx.din("cs_tm", [NT, 64])
    csT = cx.din("csT", [64, NT])
    sel = cx.din("sel", [16, 18])
    y = cx.dout("y", [NT, D])
    hA = cx.dint("hA", [NT, D])
    hB = cx.dint("hB", [NT, D])
    h_mid = cx.dint("h_mid", [NT, D])
    Ql = cx.dint("Ql", [16 * 96, NT], BF16)
    Kl = cx.dint("Kl", [D, NT], BF16)
    Vl = cx.dint("Vl", [NT, D], BF16)
    Krl = cx.dint("Krl", [32, NT], BF16)
    lfl = cx.dint("lfl", [16, NT], F32)
    Kg = cx.dint("Kg", [8, 2, 128, NT], BF16)
    Vg = cx.dint("Vg", [4, 2, 1024, D], BF16)
    Vgt = cx.dint("Vgt", [2, TAIL, D], BF16)
    Krg2 = cx.dint("Krg2", [64, NT], BF16)
    lfg2 = cx.dint("lfg2", [32, NT], F32)
    OT = cx.dint("OT", [D, NT], BF16)
    scr = {"Fq": cx.dint("Fq", [16, 3, NT], BF16), "NFk": cx.dint("NFk", [16, 3, L], BF16)}
    kv_pairs = [(Kl[p * 128:(p + 1) * 128, :], Kg[p].rearrange("r d t -> (r d) t")) for p in range(8)]
    kv_pairs += [(Vl[p * 1024:(p + 1) * 1024, :], Vg[p].rearrange("r n d -> (r n) d")) for p in range(4)]
    kv_pairs += [(Vl[NCH * 512:NT, :], Vgt.rearrange("r n d -> (r n) d"))]
    Krg = Krg2.rearrange("(r d) t -> r d t", r=2)
    lfg = lfg2.rearrange("(r d) t -> r d t", r=2)
    Q64 = Ql[0:D, :]
    pool_pass(cx, h_in, halo, h_mid, norm_mix[0], pool_w, pool_scale, bands, bhalo)
    ffn_pass(cx, h_mid, hA, norm_ffn[0], wg[0], wu[0], wd[0])
    proj_pass_qkv(cx, hA, norm_mix[1], sb_w_qkv, Q64, Kl, Vl)
    all_gather(cx, kv_pairs)
    attn_pass(cx, "sb", Q64, Kg, Vg, mask_lt, tmask_lt, OT, Vgt=Vgt)
    oproj_pass(cx, hA, OT, sb_w_o, h_mid)
    ffn_pass(cx, h_mid, hB, norm_ffn[1], wg[1], wu[1], wd[1])
    proj_pass_mla(cx, hB, norm_mix[2], w_down, q_norm, kv_norm, w_uq, w_ukv, cs_tm, csT, Ql, Kl, Krl, Vl)
    all_gather(cx, kv_pairs + [(Krl, Krg2)])
    attn_pass(cx, "mla", Ql, Kg, Vg, mask_le, tmask_le, OT, scale=float(96 ** -0.5), Krg=Krg, Vgt=Vgt)
    oproj_pass(cx, hB, OT, mla_w_o, h_mid)
    ffn_pass(cx, h_mid, hA, norm_ffn[2], wg[2], wu[2], wd[2])
    proj_pass_qkv(cx, hA, norm_mix[3], fox_w_qkvf, Q64, Kl, Vl, fox=True, b_f=fox_b_f, lfT=lfl)
    all_gather(cx, kv_pairs + [(lfl, lfg2)])
    attn_pass(cx, "fox", Q64, Kg, Vg, mask_le, tmask_le, OT, lfg=lfg, sel=sel, scr=scr, Vgt=Vgt)
    oproj_pass(cx, hA, OT, fox_w_o, h_mid)
    ffn_pass(cx, h_mid, None, norm_ffn[3], wg[3], wu[3], wd[3], final_g=g_fin, y_out=y)
    return cx.finish()


def kernel(x, meta, norm_mix, norm_ffn, pool_w, pool_scale, sb_w_qkv, sb_w_o,
           mla_w_down, mla_q_norm, mla_kv_norm, mla_w_uq, mla_w_ukv, mla_w_o,
           fox_w_qkvf, fox_b_f, fox_w_o, ffn_w_gate, ffn_w_up, ffn_w_down, final_norm):
    f = lambda a: np.ascontiguousarray(np.asarray(a, dtype=np.float32))
    x, meta = f(x), f(meta)
    bands, bhalo = band_tables()
    shared = {"norm_mix": f(norm_mix), "norm_ffn": f(norm_ffn), "pool_w": f(pool_w)[0], "pool_scale": f(pool_scale)[0],
              "wg": f(ffn_w_gate), "wu": f(ffn_w_up), "wd": f(ffn_w_down), "sb_w_qkv": f(sb_w_qkv)[0], "sb_w_o": f(sb_w_o)[0],
              "mla_w_down": f(mla_w_down)[0], "mla_q_norm": f(mla_q_norm)[0], "mla_kv_norm": f(mla_kv_norm)[0],
              "mla_w_uq": f(mla_w_uq)[0], "mla_w_ukv": f(mla_w_ukv)[0], "mla_w_o": f(mla_w_o)[0],
              "fox_w_qkvf": f(fox_w_qkvf)[0], "fox_b_f": f(fox_b_f)[0], "fox_w_o": f(fox_w_o)[0], "g_fin": f(final_norm),
              "bhalo": bhalo}
    per_rank = []
    for r in range(2):
        m_lt, t_lt = mask_tables(r, strict=True)
        m_le, t_le = mask_tables(r, strict=False)
        cs_tm, csT = rope_tables(r)
        sel = np.zeros((16, 18), np.float32)
        oc = owned_chunks(r)
        for j in range(NCH):
            sel[:, 2 * j + (oc[j] - 2 * j)] = 1.0
        sel[:, 16 + r] = 1.0
        bd = bands.copy()
        if r == 1:
            bd[:, 2] = bd[:, 0]
        per_rank.append({"mask_lt": m_lt, "tmask_lt": t_lt, "mask_le": m_le, "tmask_le": t_le, "cs_tm": cs_tm, "csT": csT,
                         "sel": sel, "bands": bd})
    in_maps = []
    for c in range(8):
        b, r = c // 2, c % 2
        hfull = np.concatenate([meta, x[b]], axis=0)
        pos = owned_positions(r)
        starts = [g * 512 for g in owned_chunks(r)] + [SEQ + TAIL * r]
        halo = np.zeros((NCH + 1, 16, D), np.float32)
        for i, s0 in enumerate(starts):
            lo = max(0, s0 - 16)
            if s0 > 0:
                halo[i, 16 - (s0 - lo):] = hfull[lo:s0]
        m = {"h_in": np.ascontiguousarray(hfull[pos]), "halo": halo}
        m.update(shared)
        m.update(per_rank[r])
        in_maps.append(m)
    res = _run(_get_nc("fused", build_fused), in_maps, "fused")
    out = np.zeros((4, SEQ, D), np.float32)
    for c in range(8):
        b, r = c // 2, c % 2
        pos = owned_positions(r)
        yv = np.asarray(res[c]["y"])
        keep = pos >= NMETA
        out[b, pos[keep] - NMETA] = yv[keep]
    return out
```
